# Optimizing a Trainium2 kernel written in Bass

```python
import math
import jax
import jax.numpy as jnp
from jax import lax
import numpy as np

D_MODEL = 1024
BATCH = 2
SEQ = 16384
DEPTH = 2

EXPAND = 2
D_INNER = EXPAND * D_MODEL
CHUNK = 128
NORM_EPS = 1e-6
S5_WIDTH = D_INNER // 2
S5_GROUP = 16
S5_GROUPS = S5_WIDTH // S5_GROUP
S5_STATE = 64
SSD_WIDTH = D_INNER - S5_WIDTH
SSD_HEAD_DIM = 64
SSD_HEADS = SSD_WIDTH // SSD_HEAD_DIM
SSD_GROUPS = 2
SSD_REP = SSD_HEADS // SSD_GROUPS
SSD_STATE = 128
SSD_CONV = 4
SSD_CONV_DIM = SSD_WIDTH + 2 * SSD_GROUPS * SSD_STATE
AB_PROJ = D_INNER + S5_WIDTH + SSD_CONV_DIM + SSD_HEADS
RET_HEADS = 4
RET_QK_DIM = D_MODEL // RET_HEADS
RET_V_DIM = D_INNER // RET_HEADS
RET_PROJ = 2 * RET_HEADS * RET_QK_DIM + 2 * D_INNER
ROPE_BASE = 10000.0
N_EVEN = (DEPTH + 1) // 2
N_ODD = DEPTH // 2

kernel_name = 's5_ssd_retention_hybrid'

F32 = jnp.float32


def rmsnorm(x, w, n_groups=1):
    xf = x.astype(F32)
    xg = xf.reshape(*x.shape[:-1], n_groups, x.shape[-1] // n_groups)
    xg = xg * lax.rsqrt(jnp.mean(xg * xg, axis=-1, keepdims=True) + NORM_EPS)
    return (xg.reshape(x.shape) * w.astype(F32)).astype(x.dtype)


def _complex_affine_combine(left, right):
    a1r, a1i, b1r, b1i = left
    a2r, a2i, b2r, b2i = right
    return (a2r * a1r - a2i * a1i,
            a2r * a1i + a2i * a1r,
            a2r * b1r - a2i * b1i + b2r,
            a2r * b1i + a2i * b1r + b2i)


def s5_mixer(u, lam_re, lam_im, log_dt, b_re, b_im, c_re, c_im, d_skip, glu_w, glu_b):
    bsz, seqlen, _ = u.shape
    n_chunks = seqlen // CHUNK
    uf = u.astype(F32)
    dt = jnp.exp(log_dt.astype(F32))[:, None]
    lr = jnp.minimum(lam_re.astype(F32), -1e-4)
    li = lam_im.astype(F32)
    mag = jnp.exp(lr * dt)
    ab_re = mag * jnp.cos(li * dt)
    ab_im = mag * jnp.sin(li * dt)
    den = lr * lr + li * li
    nr = ab_re - 1.0
    coef_re = (nr * lr + ab_im * li) / den
    coef_im = (ab_im * lr - nr * li) / den
    br = b_re.astype(F32)
    bi = b_im.astype(F32)
    bb_re = coef_re[..., None] * br - coef_im[..., None] * bi
    bb_im = coef_re[..., None] * bi + coef_im[..., None] * br
    cr = c_re.astype(F32)
    ci = c_im.astype(F32)
    st_shape = (bsz, CHUNK, S5_GROUPS, S5_STATE)
    a_re = jnp.broadcast_to(ab_re, st_shape)
    a_im = jnp.broadcast_to(ab_im, st_shape)
    u_chunks = uf.reshape(bsz, n_chunks, CHUNK, S5_GROUPS, S5_GROUP).transpose(1, 0, 2, 3, 4)

    def step(carry, u_c):
        h_re, h_im = carry
        bu_re = jnp.einsum('btgh,gph->btgp', u_c, bb_re)
        bu_im = jnp.einsum('btgh,gph->btgp', u_c, bb_im)
        acum_re, acum_im, x_re, x_im = lax.associative_scan(
            _complex_affine_combine, (a_re, a_im, bu_re, bu_im), axis=1)
        x_re = x_re + acum_re * h_re[:, None] - acum_im * h_im[:, None]
        x_im = x_im + acum_re * h_im[:, None] + acum_im * h_re[:, None]
        y = jnp.einsum('btgp,ghp->btgh', x_re, cr) - jnp.einsum('btgp,ghp->btgh', x_im, ci)
        return (x_re[:, -1], x_im[:, -1]), y

    zero = jnp.zeros((bsz, S5_GROUPS, S5_STATE), F32)
    _, ys = lax.scan(step, (zero, zero), u_chunks)
    y = ys.transpose(1, 0, 2, 3, 4).reshape(bsz, seqlen, S5_WIDTH)
    y = jax.nn.gelu(y + d_skip.astype(F32) * uf)
    return y * jax.nn.sigmoid(y @ glu_w.astype(F32) + glu_b.astype(F32))


def causal_depthwise_conv(x, w, b):
    k = w.shape[0]
    y = lax.conv_general_dilated(x, w[:, None, :], window_strides=(1,), padding=[(k - 1, 0)],
                                 dimension_numbers=('NWC', 'WIO', 'NWC'),
                                 feature_group_count=x.shape[-1])
    return y + b


def ssd_mixer(xbc, dt_raw, conv_w, conv_b, dt_bias, a_log, d_skip):
    bsz, seqlen, _ = xbc.shape
    nc = seqlen // CHUNK
    xbc = jax.nn.silu(causal_depthwise_conv(xbc.astype(F32), conv_w.astype(F32), conv_b.astype(F32)))
    gn = SSD_GROUPS * SSD_STATE
    xs = xbc[..., :SSD_WIDTH].reshape(bsz, nc, CHUNK, SSD_GROUPS, SSD_REP, SSD_HEAD_DIM)
    bm = xbc[..., SSD_WIDTH:SSD_WIDTH + gn].reshape(bsz, nc, CHUNK, SSD_GROUPS, SSD_STATE)
    cm = xbc[..., SSD_WIDTH + gn:].reshape(bsz, nc, CHUNK, SSD_GROUPS, SSD_STATE)
    dt = jax.nn.softplus(dt_raw.astype(F32) + dt_bias.astype(F32))
    a = -jnp.exp(a_log.astype(F32))
    xdt = xs * dt.reshape(bsz, nc, CHUNK, SSD_GROUPS, SSD_REP)[..., None]
    da = (dt * a).reshape(bsz, nc, CHUNK, SSD_HEADS).transpose(0, 3, 1, 2)
    a_cs = jnp.cumsum(da, axis=-1)
    causal = jnp.tril(jnp.ones((CHUNK, CHUNK), dtype=bool))
    seg = a_cs[..., :, None] - a_cs[..., None, :]
    decay_mat = jnp.exp(jnp.where(causal, seg, -jnp.inf)).reshape(
        bsz, SSD_GROUPS, SSD_REP, nc, CHUNK, CHUNK)
    cb = jnp.einsum('bclgn,bcsgn->bgcls', cm, bm)
    y_diag = jnp.einsum('bgrcls,bcsgrp->bclgrp', cb[:, :, None] * decay_mat, xdt)

    def to_bclgr(t):
        return t.reshape(bsz, SSD_GROUPS, SSD_REP, nc, CHUNK).transpose(0, 3, 4, 1, 2)

    decay_states = to_bclgr(jnp.exp(a_cs[..., -1:] - a_cs))
    states = jnp.einsum('bclgn,bclgrp->bcgrpn', bm, xdt * decay_states[..., None])
    chunk_decay = jnp.exp(a_cs[..., -1]).reshape(bsz, SSD_GROUPS, SSD_REP, nc).transpose(3, 0, 1, 2)

    def step(h, inp):
        dec, st = inp
        return h * dec[..., None, None] + st, h

    init = jnp.zeros((bsz, SSD_GROUPS, SSD_REP, SSD_HEAD_DIM, SSD_STATE), F32)
    _, prev = lax.scan(step, init, (chunk_decay, states.transpose(1, 0, 2, 3, 4, 5)))
    prev = prev.transpose(1, 0, 2, 3, 4, 5)
    y_off = jnp.einsum('bclgn,bcgrpn->bclgrp', cm, prev) * to_bclgr(jnp.exp(a_cs))[..., None]
    y = y_diag + y_off + xs * d_skip.astype(F32).reshape(SSD_GROUPS, SSD_REP)[:, :, None]
    return y.reshape(bsz, seqlen, SSD_WIDTH)


def s5_ssd_layer(h, w_in, lam_re, lam_im, log_dt, b_re, b_im, c_re, c_im, s5_d, glu_w, glu_b,
                 conv_w, conv_b, dt_bias, a_log, ssd_d, ssd_norm_w, w_out):
    p = h @ w_in
    o1 = D_INNER
    o2 = o1 + S5_WIDTH
    o3 = o2 + SSD_CONV_DIM
    z = p[..., :o1].astype(F32)
    y_a = s5_mixer(p[..., o1:o2], lam_re, lam_im, log_dt, b_re, b_im, c_re, c_im, s5_d, glu_w, glu_b)
    y_a = y_a * jax.nn.silu(z[..., :S5_WIDTH])
    y_b = ssd_mixer(p[..., o2:o3], p[..., o3:], conv_w, conv_b, dt_bias, a_log, ssd_d)
    y_b = rmsnorm(y_b * jax.nn.silu(z[..., S5_WIDTH:]), ssd_norm_w, SSD_GROUPS)
    y = jnp.concatenate([y_a, y_b], axis=-1)
    return y.astype(w_out.dtype) @ w_out


def rotary(x, cos, sin):
    x1 = x[..., ::2]
    x2 = x[..., 1::2]
    rot = jnp.stack([-x2, x1], axis=-1).reshape(x.shape)
    return x * cos[:, None] + rot * sin[:, None]


def chunkwise_retention(q, k, v):
    bsz, seqlen = q.shape[0], q.shape[1]
    nc = seqlen // CHUNK
    log_gamma = jnp.log(1.0 - 2.0 ** (-5.0 - jnp.arange(RET_HEADS, dtype=F32)))
    pos = jnp.arange(CHUNK, dtype=F32)
    rel = pos[:, None] - pos[None, :]
    intra = jnp.where(rel >= 0, jnp.exp(log_gamma[:, None, None] * jnp.maximum(rel, 0.0)), 0.0)
    qc = q.reshape(bsz, nc, CHUNK, RET_HEADS, RET_QK_DIM)
    kc = k.reshape(bsz, nc, CHUNK, RET_HEADS, RET_QK_DIM)
    vc = v.reshape(bsz, nc, CHUNK, RET_HEADS, RET_V_DIM)
    scores = jnp.einsum('bclhd,bcshd->bchls', qc, kc) * intra
    inner = jnp.einsum('bchls,bcshv->bclhv', scores, vc)
    q_decay = jnp.exp(log_gamma[None, :] * (pos[:, None] + 1.0))
    k_decay = jnp.exp(log_gamma[None, :] * (CHUNK - 1.0 - pos)[:, None])
    chunk_decay = jnp.exp(log_gamma * CHUNK)
    qd = (qc * q_decay[..., None]).transpose(1, 0, 2, 3, 4)
    kd = (kc * k_decay[..., None]).transpose(1, 0, 2, 3, 4)
    vt = vc.transpose(1, 0, 2, 3, 4)

    def step(state, inp):
        q_t, k_t, v_t = inp
        cross = jnp.einsum('blhd,bhdv->blhv', q_t, state)
        state = state * chunk_decay[:, None, None] + jnp.einsum('blhd,blhv->bhdv', k_t, v_t)
        return state, cross

    init = jnp.zeros((bsz, RET_HEADS, RET_QK_DIM, RET_V_DIM), F32)
    _, cross = lax.scan(step, init, (qd, kd, vt))
    o = inner + cross.transpose(1, 0, 2, 3, 4)
    return o.reshape(bsz, seqlen, RET_HEADS, RET_V_DIM)


def retention_layer(h, w_in, gn_w, gn_b, w_out):
    bsz, seqlen, _ = h.shape
    p = (h @ w_in).astype(F32)
    qk_w = RET_HEADS * RET_QK_DIM
    q = p[..., :qk_w].reshape(bsz, seqlen, RET_HEADS, RET_QK_DIM)
    k = p[..., qk_w:2 * qk_w].reshape(bsz, seqlen, RET_HEADS, RET_QK_DIM)
    v = p[..., 2 * qk_w:2 * qk_w + D_INNER].reshape(bsz, seqlen, RET_HEADS, RET_V_DIM)
    g = p[..., 2 * qk_w + D_INNER:]
    pos = jnp.arange(seqlen, dtype=F32)
    angle = jnp.repeat(1.0 / (ROPE_BASE ** jnp.linspace(0.0, 1.0, RET_QK_DIM // 2, dtype=F32)), 2)
    theta = pos[:, None] * angle[None, :]
    cos = jnp.cos(theta)
    sin = jnp.sin(theta)
    q = rotary(q, cos, sin)
    k = rotary(k, cos, sin) * (RET_QK_DIM ** -0.5)
    o = chunkwise_retention(q, k, v)
    mu = jnp.mean(o, axis=-1, keepdims=True)
    var = jnp.mean(jnp.square(o - mu), axis=-1, keepdims=True)
    o = ((o - mu) * lax.rsqrt(var + NORM_EPS)).reshape(bsz, seqlen, D_INNER)
    o = o * gn_w.astype(F32) + gn_b.astype(F32)
    y = jax.nn.silu(g) * o
    return y.astype(w_out.dtype) @ w_out


def setup_inputs(seed: int = 0) -> dict:
    key = jax.random.key(seed)
    ks = jax.random.split(key, 25)

    def nrm(k, shape, s):
        return jax.random.normal(k, shape, F32) * s

    dt_ssd = jnp.exp(jax.random.uniform(ks[15], (N_EVEN, SSD_HEADS), F32,
                                        minval=math.log(1e-3), maxval=math.log(1e-1)))
    return {
        'x': nrm(ks[0], (BATCH, SEQ, D_MODEL), 1.0),
        'layer_norm_w': 1.0 + nrm(ks[1], (DEPTH, D_MODEL), 0.01),
        'ab_w_in': nrm(ks[2], (N_EVEN, D_MODEL, AB_PROJ), D_MODEL ** -0.5),
        's5_lam_re': -0.5 + nrm(ks[3], (N_EVEN, S5_GROUPS, S5_STATE), 0.01),
        's5_lam_im': math.pi * jnp.arange(S5_STATE, dtype=F32) + nrm(ks[4], (N_EVEN, S5_GROUPS, S5_STATE), 0.01),
        's5_log_dt': jax.random.uniform(ks[5], (N_EVEN, S5_GROUPS), F32,
                                        minval=math.log(1e-3), maxval=math.log(1e-1)),
        's5_b_re': nrm(ks[6], (N_EVEN, S5_GROUPS, S5_STATE, S5_GROUP), (2 * S5_GROUP) ** -0.5),
        's5_b_im': nrm(ks[7], (N_EVEN, S5_GROUPS, S5_STATE, S5_GROUP), (2 * S5_GROUP) ** -0.5),
        's5_c_re': nrm(ks[8], (N_EVEN, S5_GROUPS, S5_GROUP, S5_STATE), S5_STATE ** -0.5),
        's5_c_im': nrm(ks[9], (N_EVEN, S5_GROUPS, S5_GROUP, S5_STATE), S5_STATE ** -0.5),
        's5_d': nrm(ks[10], (N_EVEN, S5_WIDTH), 1.0),
        's5_glu_w': nrm(ks[11], (N_EVEN, S5_WIDTH, S5_WIDTH), S5_WIDTH ** -0.5),
        's5_glu_b': nrm(ks[12], (N_EVEN, S5_WIDTH), 0.01),
        'ssd_conv_w': nrm(ks[13], (N_EVEN, SSD_CONV, SSD_CONV_DIM), SSD_CONV ** -0.5),
        'ssd_conv_b': nrm(ks[14], (N_EVEN, SSD_CONV_DIM), 0.01),
        'ssd_dt_bias': dt_ssd + jnp.log(-jnp.expm1(-dt_ssd)),
        'ssd_a_log': jnp.log(jax.random.uniform(ks[16], (N_EVEN, SSD_HEADS), F32, minval=1.0, maxval=16.0)),
        'ssd_d': 1.0 + nrm(ks[17], (N_EVEN, SSD_HEADS), 0.1),
        'ssd_norm_w': 1.0 + nrm(ks[18], (N_EVEN, SSD_WIDTH), 0.01),
        'ab_w_out': nrm(ks[19], (N_EVEN, D_INNER, D_MODEL), D_INNER ** -0.5),
        'ret_w_in': nrm(ks[20], (N_ODD, D_MODEL, RET_PROJ), D_MODEL ** -0.5),
        'ret_gn_w': 1.0 + nrm(ks[21], (N_ODD, D_INNER), 0.01),
        'ret_gn_b': nrm(ks[22], (N_ODD, D_INNER), 0.01),
        'ret_w_out': nrm(ks[23], (N_ODD, D_INNER, D_MODEL), D_INNER ** -0.5),
        'final_norm_w': 1.0 + nrm(ks[24], (D_MODEL,), 0.01),
    }


def reference(x, layer_norm_w, ab_w_in, s5_lam_re, s5_lam_im, s5_log_dt, s5_b_re, s5_b_im,
              s5_c_re, s5_c_im, s5_d, s5_glu_w, s5_glu_b, ssd_conv_w, ssd_conv_b, ssd_dt_bias,
              ssd_a_log, ssd_d, ssd_norm_w, ab_w_out, ret_w_in, ret_gn_w, ret_gn_b, ret_w_out,
              final_norm_w):
    h = x
    for i in range(DEPTH):
        hn = rmsnorm(h, layer_norm_w[i])
        j = i // 2
        if i % 2 == 0:
            y = s5_ssd_layer(hn, ab_w_in[j], s5_lam_re[j], s5_lam_im[j], s5_log_dt[j],
                             s5_b_re[j], s5_b_im[j], s5_c_re[j], s5_c_im[j], s5_d[j],
                             s5_glu_w[j], s5_glu_b[j], ssd_conv_w[j], ssd_conv_b[j],
                             ssd_dt_bias[j], ssd_a_log[j], ssd_d[j], ssd_norm_w[j], ab_w_out[j])
        else:
            y = retention_layer(hn, ret_w_in[j], ret_gn_w[j], ret_gn_b[j], ret_w_out[j])
        h = h + y.astype(h.dtype)
    return rmsnorm(h, final_norm_w)
```

```python
import math
import numpy as np
from contextlib import ExitStack
import concourse.bass as bass
import concourse.mybir as mybir
import ml_dtypes


F32 = mybir.dt.float32
BF16 = mybir.dt.bfloat16
AF = mybir.ActivationFunctionType
ALU = mybir.AluOpType
AX = mybir.AxisListType

COMPUTE = ("tensor", "vector", "scalar", "gpsimd")
NDMA_SEMS = 12


class Prog:
    def __init__(self, nc, es):
        self.nc = nc
        self.es = es
        self.ops = []
        self.n_sb = 0

    def sb(self, shape, dtype=F32, name=None):
        self.n_sb += 1
        t = self.es.enter_context(self.nc.sbuf_tensor("t_" + (name or f"sb{self.n_sb}"), list(shape), dtype))
        return t

    def ps(self, shape, dtype=F32, name=None):
        self.n_sb += 1
        t = self.es.enter_context(self.nc.psum_tensor("p_" + (name or f"ps{self.n_sb}"), list(shape), dtype))
        return t

    def op(self, eng, fn, reads=(), writes=()):
        self.ops.append(dict(eng=eng, fn=fn, reads=tuple(reads), writes=tuple(writes), dma=False))

    def dma(self, q, out, in_, reads=(), writes=(), **kw):
        self.ops.append(dict(eng=q, fn=(lambda e: e.dma_start(out=out, in_=in_, **kw)),
                             reads=tuple(reads), writes=tuple(writes), dma=True))

    def emit(self):
        nc = self.nc
        ops = self.ops
        last_w = {}
        readers = {}
        deps = [set() for _ in ops]
        for i, o in enumerate(ops):
            d = deps[i]
            for k in o["reads"]:
                if k in last_w:
                    d.add(last_w[k])
            for k in o["writes"]:
                if k in last_w:
                    d.add(last_w[k])
                for r in readers.get(k, ()):
                    d.add(r)
            d.discard(i)
            if o["eng"] == "tensor":
                for j in list(d):
                    if ops[j]["eng"] == "tensor":
                        d.discard(j)
            for k in o["writes"]:
                last_w[k] = i
                readers[k] = set()
            for k in o["reads"]:
                readers.setdefault(k, set()).add(i)
        engs = sorted(set(o["eng"] for o in ops))
        dma_slot_last = {}
        dma_rr = {e: 0 for e in engs}
        for i, o in enumerate(ops):
            if o["dma"]:
                q = o["eng"]
                s = dma_rr[q] % NDMA_SEMS
                dma_rr[q] += 1
                o["slot"] = (q, s)
                if (q, s) in dma_slot_last:
                    deps[i].add(dma_slot_last[(q, s)])
                dma_slot_last[(q, s)] = i
        needed = set()
        for d in deps:
            needed |= d
        for i, o in enumerate(ops):
            if o["dma"]:
                needed.add(i)
        cnt = {}
        tok = {}
        for i, o in enumerate(ops):
            if i not in needed:
                continue
            if o["dma"]:
                key = ("dma",) + o["slot"]
                cnt[key] = cnt.get(key, 0) + 16
            else:
                key = ("eng", o["eng"])
                cnt[key] = cnt.get(key, 0) + 1
            tok[i] = (key, cnt[key])
        sems = {}
        for key in cnt:
            sems[key] = self.es.enter_context(nc.semaphore("s_" + "_".join(str(x) for x in key)))
        block = self.es.enter_context(nc.Block())
        final_dma = {}
        for i, o in enumerate(ops):
            if o["dma"]:
                final_dma[tok[i][0]] = tok[i][1]

        def make(engname):
            def body(e):
                waited = {}
                for i, o in enumerate(ops):
                    if o["eng"] != engname:
                        continue
                    for dpi in sorted(deps[i]):
                        key, val = tok[dpi]
                        if waited.get(key, 0) < val:
                            e.wait_ge(sems[key], val)
                            waited[key] = val
                    ins = o["fn"](e)
                    if i in tok:
                        key, val = tok[i]
                        ins.then_inc(sems[key], 16 if o["dma"] else 1)
                if engname == "sync":
                    for key, val in final_dma.items():
                        if waited.get(key, 0) < val:
                            e.wait_ge(sems[key], val)
            return body

        allengs = set(engs) | {"sync"}
        for en in allengs:
            getattr(block, en)(make(en))
        return len(ops)


EPS = 1e-6


def load_weight_bf16(P, W_dram, K, N, name):
    nc = P.nc
    KT = K // 128
    wb = P.sb([128, KT, N], BF16, name=name)
    CH = 1024
    if not hasattr(P, "_wstg"):
        P._wstg = [P.sb([128, CH], F32, name=f"wstg{i}") for i in range(3)]
        P._wn = 0
    for kt in range(KT):
        eng = ["vector", "gpsimd"][kt % 2]
        for c0 in range(0, N, CH):
            cw = min(CH, N - c0)
            si = P._wn % 3
            s = P._wstg[si]
            P._wn += 1
            P.dma("sync", s[:, 0:cw], W_dram[kt * 128:(kt + 1) * 128, c0:c0 + cw], writes=[("wstg", si)])
            P.op(eng, (lambda e, s=s, kt=kt, c0=c0, cw=cw: e.tensor_copy(out=wb[:, kt, c0:c0 + cw], in_=s[:, 0:cw])),
                 reads=[("wstg", si)], writes=[(name, kt)])
    return wb


def bcast_row(P, vec_dram, N, name):
    t = P.sb([128, N], F32, name=name)
    P.dma("sync", t[:, :], vec_dram.partition_broadcast(128), writes=[(name,)])
    return t


def rms_norm_T(P, x_tile, xkey, lnw_b, lnwkey, ident, st, i, D=1024):
    nc = P.nc
    b = i % 2
    KT = D // 128
    junk, ss, rstd, hn, hnT, tp = st["junk"][b], st["ss"][b], st["rstd"][b], st["hn"][b], st["hnT"][b], st["tp"][b]
    tpk = st["tpkeys"][b]
    P.op("scalar", lambda e: e.activation(out=junk[:, :], in_=x_tile, func=AF.Square, accum_out=ss[:, 0:1]),
         reads=[xkey], writes=[("junk", b), ("ss", b)])
    P.op("vector", lambda e: e.tensor_scalar(out=rstd[:, 0:1], in0=ss[:, 0:1], scalar1=1.0 / D, scalar2=EPS,
                                             op0=ALU.mult, op1=ALU.add),
         reads=[("ss", b)], writes=[("rstd", b)])
    P.op("scalar", lambda e: e.activation(out=rstd[:, 0:1], in_=rstd[:, 0:1], func=AF.Sqrt),
         reads=[("rstd", b)], writes=[("rstd", b)])
    P.op("vector", lambda e: e.reciprocal(out=rstd[:, 0:1], in_=rstd[:, 0:1]),
         reads=[("rstd", b)], writes=[("rstd", b)])
    P.op("vector", lambda e: e.scalar_tensor_tensor(out=hn[:, :], in0=x_tile, scalar=rstd[:, 0:1], in1=lnw_b[:, :],
                                                    op0=ALU.mult, op1=ALU.mult),
         reads=[xkey, ("rstd", b), lnwkey], writes=[("hn", b)])
    for kt in range(KT):
        P.op("tensor", (lambda e, kt=kt: e.transpose(out=tp[:, kt * 128:(kt + 1) * 128],
                                                     in_=hn[:, kt * 128:(kt + 1) * 128], identity=ident[:, :])),
             reads=[("hn", b), ("ident",)], writes=[tpk])
    P.op("scalar", lambda e: e.copy(out=hnT[:, :], in_=tp[:, :]), reads=[tpk], writes=[("hnT", b)])
    return hnT, ("hnT", b)


def alloc_norm_state(P, D=1024, single_tp=False):
    st = {}
    st["junk"] = [P.sb([128, D], BF16, name=f"junk{i}") for i in range(2)]
    st["ss"] = [P.sb([128, 1], F32, name=f"ss{i}") for i in range(2)]
    st["rstd"] = [P.sb([128, 1], F32, name=f"rstd{i}") for i in range(2)]
    st["hn"] = [P.sb([128, D], BF16, name=f"hn{i}") for i in range(2)]
    st["hnT"] = [P.sb([128, D], BF16, name=f"hnT{i}") for i in range(2)]
    if single_tp:
        t = P.ps([128, D], BF16, name="tp0")
        st["tp"] = [t, t]
        st["tpkeys"] = [("tp", 0), ("tp", 0)]
    else:
        st["tp"] = [P.ps([128, D], BF16, name=f"tp{i}") for i in range(2)]
        st["tpkeys"] = [("tp", 0), ("tp", 1)]
    return st


def build_proj(T, NC):
    nc = bass.Bass("TRN2", target_bir_lowering=False)
    x = nc.dram_tensor("x", [T, 1024], F32, kind="ExternalInput").ap()
    lnw = nc.dram_tensor("lnw", [1024], F32, kind="ExternalInput").ap()
    W = nc.dram_tensor("w", [1024, NC], F32, kind="ExternalInput").ap()
    ident_d = nc.dram_tensor("ident_in", [128, 128], F32, kind="ExternalInput").ap()
    out = nc.dram_tensor("out", [T, NC], F32, kind="ExternalOutput").ap()
    with ExitStack() as es:
        P = Prog(nc, es)
        identf = P.sb([128, 128], F32, name="identf")
        ident = P.sb([128, 128], BF16, name="ident")
        P.dma("sync", identf[:, :], ident_d[:, :], writes=[("identf",)])
        P.op("vector", lambda e: e.tensor_copy(out=ident[:, :], in_=identf[:, :]), reads=[("identf",)], writes=[("ident",)])
        lnw_b = bcast_row(P, lnw, 1024, "lnwb")
        wb = load_weight_bf16(P, W, 1024, NC, "wb")
        st = alloc_norm_state(P)
        xs = [P.sb([128, 1024], F32, name=f"xs{i}") for i in range(2)]
        NT = (NC + 511) // 512
        acc = [P.ps([128, 512], F32, name=f"acc{i}") for i in range(4)]
        ot = [P.sb([128, NC], F32, name=f"ot{i}") for i in range(2)]
        nchunks = T // 128
        na = 0
        for c in range(nchunks):
            b = c % 2
            P.dma("sync", xs[b][:, :], x[c * 128:(c + 1) * 128, :], writes=[("xs", b)])
            hnT, hkey = rms_norm_T(P, xs[b][:, :], ("xs", b), lnw_b, ("lnwb",), ident, st, c)
            for nt in range(NT):
                n0 = nt * 512
                nw = min(512, NC - n0)
                a = acc[na % 4]
                akey = ("acc", na % 4)
                for kt in range(8):
                    P.op("tensor", (lambda e, a=a, kt=kt, n0=n0, nw=nw, hnT=hnT: e.matmul(
                        a[:, 0:nw], lhsT=hnT[:, kt * 128:(kt + 1) * 128], rhs=wb[:, kt, n0:n0 + nw],
                        start=(kt == 0), stop=(kt == 7))),
                        reads=[hkey, ("wb", kt)], writes=[akey])
                if na % 2 == 0:
                    P.op("scalar", (lambda e, a=a, n0=n0, nw=nw, b=b: e.copy(out=ot[b][:, n0:n0 + nw], in_=a[:, 0:nw])),
                         reads=[akey], writes=[("ot", b, nt)])
                else:
                    P.op("vector", (lambda e, a=a, n0=n0, nw=nw, b=b: e.tensor_copy(out=ot[b][:, n0:n0 + nw], in_=a[:, 0:nw])),
                         reads=[akey], writes=[("ot", b, nt)])
                na += 1
            P.dma("gpsimd", out[c * 128:(c + 1) * 128, :], ot[b][:, :],
                  reads=[("ot", b, nt) for nt in range(NT)])
        P.emit()
    return nc


def setup_ident(P, ident_d):
    identf = P.sb([128, 128], F32, name="identf")
    ident = P.sb([128, 128], BF16, name="ident")
    P.dma("sync", identf[:, :], ident_d[:, :], writes=["identf"])
    P.op("vector", lambda e: e.tensor_copy(out=ident[:, :], in_=identf[:, :]), reads=["identf"], writes=[("ident",)])
    return ident


def out_block(P, c, ycat, ycat_keys, hres, hreskey, wout, ident, bufs, nw_b, nwkey, h_out_d, n_out_d, n_dtype_bf16):
    b = c % 2
    tpa, tpb, acc0, acc1 = bufs["tpa"], bufs["tpb"], bufs["acc0"], bufs["acc1"]
    yT = bufs["yT"][b]
    for kt in range(16):
        tp = tpa if kt < 8 else tpb
        P.op("tensor", (lambda e, kt=kt, tp=tp: e.transpose(out=tp[:, (kt % 8) * 128:(kt % 8 + 1) * 128],
                                                            in_=ycat[:, kt * 128:(kt + 1) * 128], identity=ident[:, :])),
             reads=list(ycat_keys) + [("ident",)], writes=["tpa" if kt < 8 else "tpb"])
    P.op("scalar", lambda e: e.copy(out=yT[:, 0:1024], in_=tpa[:, :]), reads=["tpa"], writes=[("yT", b, 0)])
    P.op("vector", lambda e: e.tensor_copy(out=yT[:, 1024:2048], in_=tpb[:, :]), reads=["tpb"], writes=[("yT", b, 1)])
    for nt, acc, akey in ((0, acc0, "acc0"), (1, acc1, "acc1")):
        for kt in range(16):
            P.op("tensor", (lambda e, nt=nt, acc=acc, kt=kt: e.matmul(
                acc[:, :], lhsT=yT[:, kt * 128:(kt + 1) * 128], rhs=wout[:, kt, nt * 512:(nt + 1) * 512],
                start=(kt == 0), stop=(kt == 15))),
                reads=[("yT", b, kt // 8), ("wout", kt)], writes=[akey])
    hn_ = bufs["hnew"][b]
    P.op("vector", lambda e: e.tensor_tensor(out=hn_[:, 0:512], in0=acc0[:, :], in1=hres[:, 0:512], op=ALU.add),
         reads=["acc0", hreskey], writes=[("hnew", b, 0)])
    P.op("vector", lambda e: e.tensor_tensor(out=hn_[:, 512:1024], in0=acc1[:, :], in1=hres[:, 512:1024], op=ALU.add),
         reads=["acc1", hreskey], writes=[("hnew", b, 1)])
    hk = [("hnew", b, 0), ("hnew", b, 1)]
    tsl = slice(c * 128, (c + 1) * 128)
    if h_out_d is not None:
        P.dma("gpsimd", h_out_d[tsl, :], hn_[:, :], reads=hk)
    junk, ss = bufs["junk"], bufs["ss2"][b]
    P.op("scalar", lambda e: e.activation(out=junk[:, :], in_=hn_[:, :], func=AF.Square, accum_out=ss[:, 0:1]),
         reads=hk, writes=["junk_o", ("ss2", b)])
    P.op("vector", lambda e: e.tensor_scalar(out=ss[:, 0:1], in0=ss[:, 0:1], scalar1=1.0 / 1024, scalar2=EPS,
                                             op0=ALU.mult, op1=ALU.add), reads=[("ss2", b)], writes=[("ss2", b)])
    P.op("scalar", lambda e: e.activation(out=ss[:, 0:1], in_=ss[:, 0:1], func=AF.Sqrt), reads=[("ss2", b)], writes=[("ss2", b)])
    P.op("vector", lambda e: e.reciprocal(out=ss[:, 0:1], in_=ss[:, 0:1]), reads=[("ss2", b)], writes=[("ss2", b)])
    no = bufs["nout"][b]
    P.op("vector", lambda e: e.scalar_tensor_tensor(out=no[:, :], in0=hn_[:, :], scalar=ss[:, 0:1], in1=nw_b[:, :],
                                                    op0=ALU.mult, op1=ALU.mult),
         reads=hk + [("ss2", b), nwkey], writes=[("nout", b)])
    P.dma("gpsimd", n_out_d[tsl, :], no[:, :], reads=[("nout", b)])


def alloc_out_bufs(P, n_bf16):
    bufs = {}
    bufs["tpa"] = P.ps([128, 1024], BF16, name="tpa")
    bufs["tpb"] = P.ps([128, 1024], BF16, name="tpb")
    bufs["acc0"] = P.ps([128, 512], F32, name="acc0")
    bufs["acc1"] = P.ps([128, 512], F32, name="acc1")
    bufs["yT"] = [P.sb([128, 2048], BF16, name=f"yT{i}") for i in range(2)]
    bufs["hnew"] = [P.sb([128, 1024], F32, name=f"hnew{i}") for i in range(2)]
    bufs["junk"] = P.sb([128, 1024], BF16, name="junk_o")
    bufs["ss2"] = [P.sb([128, 1], F32, name=f"ss2_{i}") for i in range(2)]
    bufs["nout"] = [P.sb([128, 1024], BF16 if n_bf16 else F32, name=f"nout{i}") for i in range(2)]
    return bufs


def build_outD(T):
    nc = bass.Bass("TRN2", target_bir_lowering=False)
    y_d = nc.dram_tensor("y", [T, 2048], BF16, kind="ExternalInput").ap()
    h_d = nc.dram_tensor("h", [T, 1024], F32, kind="ExternalInput").ap()
    w_d = nc.dram_tensor("wout", [2048, 1024], F32, kind="ExternalInput").ap()
    fnw_d = nc.dram_tensor("fnw", [1024], F32, kind="ExternalInput").ap()
    ident_d = nc.dram_tensor("ident_in", [128, 128], F32, kind="ExternalInput").ap()
    o_d = nc.dram_tensor("out", [T, 1024], F32, kind="ExternalOutput").ap()
    with ExitStack() as es:
        P = Prog(nc, es)
        ident = setup_ident(P, ident_d)
        fnw = bcast_row(P, fnw_d, 1024, "fnw")
        wout = load_weight_bf16(P, w_d, 2048, 1024, "wout")
        bufs = alloc_out_bufs(P, False)
        ys = [P.sb([128, 2048], BF16, name=f"ys{i}") for i in range(2)]
        hs = [P.sb([128, 1024], F32, name=f"hs{i}") for i in range(2)]
        for c in range(T // 128):
            b = c % 2
            tsl = slice(c * 128, (c + 1) * 128)
            P.dma("sync", ys[b][:, :], y_d[tsl, :], writes=[("ys", b)])
            P.dma("sync", hs[b][:, :], h_d[tsl, :], writes=[("hs", b)])
            out_block(P, c, ys[b], [("ys", b)], hs[b], ("hs", b), wout, ident, bufs, fnw, ("fnw",), None, o_d, False)
        P.emit()
    return nc


def build_outB(T):
    nc = bass.Bass("TRN2", target_bir_lowering=False)
    x_d = nc.dram_tensor("x", [T, 1024], F32, kind="ExternalInput").ap()
    ya_d = nc.dram_tensor("ya", [T, 1024], BF16, kind="ExternalInput").ap()
    yb_d = nc.dram_tensor("yb", [T, 1024], BF16, kind="ExternalInput").ap()
    lnw0_d = nc.dram_tensor("lnw0", [1024], F32, kind="ExternalInput").ap()
    lnw1_d = nc.dram_tensor("lnw1", [1024], F32, kind="ExternalInput").ap()
    wz_d = nc.dram_tensor("wz", [1024, 2048], F32, kind="ExternalInput").ap()
    glw_d = nc.dram_tensor("gluw", [1024, 1024], F32, kind="ExternalInput").ap()
    glb_d = nc.dram_tensor("glub", [1024], F32, kind="ExternalInput").ap()
    snw_d = nc.dram_tensor("snw", [1024], F32, kind="ExternalInput").ap()
    w_d = nc.dram_tensor("wout", [2048, 1024], F32, kind="ExternalInput").ap()
    ident_d = nc.dram_tensor("ident_in", [128, 128], F32, kind="ExternalInput").ap()
    h1_d = nc.dram_tensor("h1", [T, 1024], F32, kind="ExternalOutput").ap()
    hn1_d = nc.dram_tensor("hn1", [T, 1024], BF16, kind="ExternalOutput").ap()
    with ExitStack() as es:
        P = Prog(nc, es)
        ident = setup_ident(P, ident_d)
        lnw0 = bcast_row(P, lnw0_d, 1024, "lnw0")
        lnw1 = bcast_row(P, lnw1_d, 1024, "lnw1")
        glb = bcast_row(P, glb_d, 1024, "glb")
        snw = bcast_row(P, snw_d, 1024, "snw")
        wz = load_weight_bf16(P, wz_d, 1024, 2048, "wz")
        glw = load_weight_bf16(P, glw_d, 1024, 1024, "glw")
        wout = load_weight_bf16(P, w_d, 2048, 1024, "wout")
        st = alloc_norm_state(P)
        bufs = alloc_out_bufs(P, True)
        zps = [P.ps([128, 512], F32, name=f"zps{i}") for i in range(2)]
        xs = [P.sb([128, 1024], F32, name=f"xs{i}") for i in range(2)]
        ya = [P.sb([128, 1024], BF16, name=f"ya{i}") for i in range(2)]
        yb = [P.sb([128, 1024], BF16, name=f"yb{i}") for i in range(2)]
        yaT = [P.sb([128, 1024], BF16, name=f"yaT{i}") for i in range(2)]
        sz = [P.sb([128, 2048], F32, name=f"sz{i}") for i in range(2)]
        sig = [P.sb([128, 1024], F32, name=f"sig{i}") for i in range(2)]
        tb = [P.sb([128, 1024], F32, name=f"tb{i}") for i in range(2)]
        ycat = [P.sb([128, 2048], BF16, name=f"ycat{i}") for i in range(2)]
        gst = [P.sb([128, 2], F32, name=f"gst{i}") for i in range(2)]
        junk2 = P.sb([128, 512], BF16, name="junk2")
        nz = 0
        for c in range(T // 128):
            b = c % 2
            tsl = slice(c * 128, (c + 1) * 128)
            P.dma("sync", xs[b][:, :], x_d[tsl, :], writes=[("xs", b)])
            P.dma("sync", ya[b][:, :], ya_d[tsl, :], writes=[("ya", b)])
            P.dma("sync", yb[b][:, :], yb_d[tsl, :], writes=[("yb", b)])
            hnT, hkey = rms_norm_T(P, xs[b][:, :], ("xs", b), lnw0, ("lnw0",), ident, st, c)
            for nt in range(4):
                zp = zps[nz % 2]
                zk = ("zps", nz % 2)
                nz += 1
                for kt in range(8):
                    P.op("tensor", (lambda e, zp=zp, kt=kt, nt=nt, hnT=hnT: e.matmul(
                        zp[:, :], lhsT=hnT[:, kt * 128:(kt + 1) * 128], rhs=wz[:, kt, nt * 512:(nt + 1) * 512],
                        start=(kt == 0), stop=(kt == 7))), reads=[hkey, ("wz", kt)], writes=[zk])
                P.op("scalar", (lambda e, zp=zp, nt=nt, b=b: e.activation(out=sz[b][:, nt * 512:(nt + 1) * 512], in_=zp[:, :], func=AF.Silu)),
                     reads=[zk], writes=[("sz", b, nt)])
            tpg = st["tp"][b]
            for kt in range(8):
                P.op("tensor", (lambda e, kt=kt, b=b, tpg=tpg: e.transpose(out=tpg[:, kt * 128:(kt + 1) * 128],
                                                                        in_=ya[b][:, kt * 128:(kt + 1) * 128], identity=ident[:, :])),
                     reads=[("ya", b), ("ident",)], writes=[("tp", b)])
            P.op("scalar", lambda e, b=b, tpg=tpg: e.copy(out=yaT[b][:, :], in_=tpg[:, :]), reads=[("tp", b)], writes=[("yaT", b)])
            for nt in range(2):
                zp = zps[nz % 2]
                zk = ("zps", nz % 2)
                nz += 1
                for kt in range(8):
                    P.op("tensor", (lambda e, zp=zp, kt=kt, nt=nt, b=b: e.matmul(
                        zp[:, :], lhsT=yaT[b][:, kt * 128:(kt + 1) * 128], rhs=glw[:, kt, nt * 512:(nt + 1) * 512],
                        start=(kt == 0), stop=(kt == 7))), reads=[("yaT", b), ("glw", kt)], writes=[zk])
                P.op("vector", (lambda e, zp=zp, nt=nt, b=b: e.tensor_tensor(out=sig[b][:, nt * 512:(nt + 1) * 512], in0=zp[:, :],
                                                                          in1=glb[:, nt * 512:(nt + 1) * 512], op=ALU.add)),
                     reads=[zk, ("glb",)], writes=[("sig", b, nt)])
                P.op("scalar", (lambda e, nt=nt, b=b: e.activation(out=sig[b][:, nt * 512:(nt + 1) * 512],
                                                                 in_=sig[b][:, nt * 512:(nt + 1) * 512], func=AF.Sigmoid)),
                     reads=[("sig", b, nt)], writes=[("sig", b, nt)])
                P.op("gpsimd", (lambda e, nt=nt, b=b: e.tensor_tensor(out=sig[b][:, nt * 512:(nt + 1) * 512],
                                                                    in0=sig[b][:, nt * 512:(nt + 1) * 512],
                                                                    in1=sz[b][:, nt * 512:(nt + 1) * 512], op=ALU.mult)),
                     reads=[("sig", b, nt), ("sz", b, nt)], writes=[("sig", b, nt)])
                P.op("gpsimd", (lambda e, nt=nt, b=b: e.tensor_tensor(out=ycat[b][:, nt * 512:(nt + 1) * 512],
                                                                    in0=sig[b][:, nt * 512:(nt + 1) * 512],
                                                                    in1=ya[b][:, nt * 512:(nt + 1) * 512], op=ALU.mult)),
                     reads=[("sig", b, nt), ("ya", b)], writes=[("ycat", b, nt)])
            for g in range(2):
                gs = slice(g * 512, (g + 1) * 512)
                P.op("vector", (lambda e, g=g, gs=gs, b=b: e.tensor_tensor(out=tb[b][:, gs], in0=yb[b][:, gs],
                                                                         in1=sz[b][:, 1024 + g * 512:1536 + g * 512], op=ALU.mult)),
                     reads=[("yb", b), ("sz", b, 2 + g)], writes=[("tb", b, g)])
                P.op("scalar", (lambda e, g=g, gs=gs, b=b: e.activation(out=junk2[:, :], in_=tb[b][:, gs], func=AF.Square,
                                                                      accum_out=gst[b][:, g:g + 1])),
                     reads=[("tb", b, g)], writes=["junk2", ("gst", b, g)])
                P.op("vector", (lambda e, g=g, b=b: e.tensor_scalar(out=gst[b][:, g:g + 1], in0=gst[b][:, g:g + 1], scalar1=1.0 / 512,
                                                                  scalar2=EPS, op0=ALU.mult, op1=ALU.add)),
                     reads=[("gst", b, g)], writes=[("gst", b, g)])
                P.op("scalar", (lambda e, g=g, b=b: e.activation(out=gst[b][:, g:g + 1], in_=gst[b][:, g:g + 1], func=AF.Sqrt)),
                     reads=[("gst", b, g)], writes=[("gst", b, g)])
                P.op("vector", (lambda e, g=g, b=b: e.reciprocal(out=gst[b][:, g:g + 1], in_=gst[b][:, g:g + 1])),
                     reads=[("gst", b, g)], writes=[("gst", b, g)])
                P.op("vector", (lambda e, g=g, gs=gs, b=b: e.scalar_tensor_tensor(
                    out=ycat[b][:, 1024 + g * 512:1536 + g * 512], in0=tb[b][:, gs], scalar=gst[b][:, g:g + 1], in1=snw[:, gs],
                    op0=ALU.mult, op1=ALU.mult)),
                    reads=[("tb", b, g), ("gst", b, g), ("snw",)], writes=[("ycat", b, 2 + g)])
            out_block(P, c, ycat[b], [("ycat", b, i) for i in range(4)], xs[b], ("xs", b), wout, ident, bufs,
                      lnw1, ("lnw1",), h1_d, hn1_d, True)
        P.emit()
    return nc


RET_EPS = 1e-6


def ret_consts(head):
    T = 128
    lg = np.log(1.0 - 2.0 ** (-5.0 - head))
    pos = np.arange(T, dtype=np.float64)
    rel = pos[None, :] - pos[:, None]
    intraT = np.where(rel >= 0, np.exp(lg * np.maximum(rel, 0.0)), 0.0) / 16.0
    gq = np.exp(lg * (pos + 1.0))[None, :].repeat(128, 0)
    kdec = (np.exp(lg * (T - 1.0 - pos)) / 16.0)[:, None]
    cd = np.exp(lg * T)
    return dict(intraT=intraT.astype(np.float32), gq=gq.astype(np.float32), kdec=kdec.astype(np.float32), cdv=np.full((128, 1), cd, np.float32))


def rope_tables(L):
    pos = np.arange(L, dtype=np.float32)
    angle = (1.0 / (10000.0 ** np.linspace(0.0, 1.0, 128, dtype=np.float32))).astype(np.float32)
    theta = (pos[None, :] * angle[:, None]).astype(np.float32)
    return np.cos(theta).astype(np.float32), np.sin(theta).astype(np.float32)


def build_ret(L):
    nc = bass.Bass("TRN2", target_bir_lowering=False)
    hn_d = nc.dram_tensor("hn", [L, 1024], BF16, kind="ExternalInput").ap()
    W = nc.dram_tensor("w", [1024, 1536], F32, kind="ExternalInput").ap()
    cos_d = nc.dram_tensor("cosT", [128, L], F32, kind="ExternalInput").ap()
    sin_d = nc.dram_tensor("sinT", [128, L], F32, kind="ExternalInput").ap()
    intra_d = nc.dram_tensor("intraT", [128, 128], F32, kind="ExternalInput").ap()
    gq_d = nc.dram_tensor("gq", [128, 128], F32, kind="ExternalInput").ap()
    kdec_d = nc.dram_tensor("kdec", [128, 1], F32, kind="ExternalInput").ap()
    cdv_d = nc.dram_tensor("cdv", [128, 1], F32, kind="ExternalInput").ap()
    gnw_d = nc.dram_tensor("gnw", [512], F32, kind="ExternalInput").ap()
    gnb_d = nc.dram_tensor("gnb", [512], F32, kind="ExternalInput").ap()
    ident_d = nc.dram_tensor("ident_in", [128, 128], F32, kind="ExternalInput").ap()
    y_d = nc.dram_tensor("y", [L, 512], BF16, kind="ExternalOutput").ap()
    with ExitStack() as es:
        P = Prog(nc, es)
        identf = P.sb([128, 128], F32, name="identf")
        ident = P.sb([128, 128], BF16, name="ident")
        P.dma("sync", identf[:, :], ident_d[:, :], writes=["identf"])
        P.op("vector", lambda e: e.tensor_copy(out=ident[:, :], in_=identf[:, :]), reads=["identf"], writes=["ident"])
        intraT = P.sb([128, 128], F32, name="intraT")
        P.dma("sync", intraT[:, :], intra_d[:, :], writes=["intraT"])
        gq = P.sb([128, 128], F32, name="gq")
        P.dma("sync", gq[:, :], gq_d[:, :], writes=["gq"])
        kdec = P.sb([128, 1], F32, name="kdec")
        P.dma("sync", kdec[:, :], kdec_d[:, :], writes=["kdec"])
        cdv = P.sb([128, 1], F32, name="cdv")
        P.dma("sync", cdv[:, :], cdv_d[:, :], writes=["cdv"])
        gnw = bcast_row(P, gnw_d, 512, "gnw")
        gnb = bcast_row(P, gnb_d, 512, "gnb")
        wb = load_weight_bf16(P, W, 1024, 1536, "wb")
        S = P.sb([128, 2, 512], F32, name="S")
        Sb = P.sb([128, 2, 512], BF16, name="Sb")
        P.op("vector", lambda e: e.memset(S[:, :, :], 0.0), writes=["S0", "S1"])
        P.op("gpsimd", lambda e: e.memset(Sb[:, :, :], 0.0), writes=["Sb0", "Sb1"])
        hn = [P.sb([128, 1024], BF16, name=f"hn{i}") for i in range(2)]
        hnT = [P.sb([128, 1024], BF16, name=f"hnT{i}") for i in range(2)]
        cs = [P.sb([128, 128], F32, name=f"cs{i}") for i in range(2)]
        sn = [P.sb([128, 128], F32, name=f"sn{i}") for i in range(2)]
        t1 = P.sb([128, 2, 128], F32, name="t1")
        t2 = P.sb([128, 2, 128], F32, name="t2")
        qkr = [P.sb([128, 4, 128], BF16, name=f"qkr{i}") for i in range(2)]
        qd = [P.sb([128, 2, 128], BF16, name=f"qd{i}") for i in range(2)]
        smT = [P.sb([128, 128], BF16, name=f"smT{i}") for i in range(2)]
        kd = [P.sb([128, 256], BF16, name=f"kd{i}") for i in range(2)]
        vb = [P.sb([128, 512], BF16, name=f"vb{i}") for i in range(2)]
        sg = [P.sb([128, 512], F32, name=f"sg{i}") for i in range(2)]
        on = [P.sb([128, 512], F32, name=f"on{i}") for i in range(2)]
        yb = [P.sb([128, 512], BF16, name=f"yb{i}") for i in range(2)]
        junk = P.sb([128, 512], BF16, name="junk")
        stat = [P.sb([128, 8], F32, name=f"stat{i}") for i in range(2)]
        tp = P.ps([128, 1024], BF16, name="tp")
        qk = P.ps([128, 4, 128], F32, name="qk")
        vps = P.ps([128, 512], F32, name="vps")
        gps = P.ps([128, 512], F32, name="gps")
        sc = P.ps([128, 128], F32, name="sc")
        kT = P.ps([128, 256], BF16, name="kT")
        ops_ = P.ps([128, 512], F32, name="ops")
        dS = P.ps([128, 512], F32, name="dS")

        nchunks = L // 128
        for c in range(nchunks):
            b = c % 2
            tsl = slice(c * 128, (c + 1) * 128)
            P.dma("sync", hn[b][:, :], hn_d[tsl, :], writes=[("hn", b)])
            P.dma("sync", cs[b][:, :], cos_d[:, tsl], writes=[("cs", b)])
            P.dma("sync", sn[b][:, :], sin_d[:, tsl], writes=[("sn", b)])
            for kt in range(8):
                P.op("tensor", (lambda e, kt=kt, b=b: e.transpose(out=tp[:, kt * 128:(kt + 1) * 128],
                                                                   in_=hn[b][:, kt * 128:(kt + 1) * 128], identity=ident[:, :])),
                     reads=[("hn", b), "ident"], writes=["tp"])
            P.op("scalar", lambda e, b=b: e.copy(out=hnT[b][:, :], in_=tp[:, :]), reads=["tp"], writes=[("hnT", b)])
            for ct in range(4):
                for kt in range(8):
                    P.op("tensor", (lambda e, ct=ct, kt=kt, b=b: e.matmul(
                        qk[:, ct, :], lhsT=wb[:, kt, ct * 128:(ct + 1) * 128], rhs=hnT[b][:, kt * 128:(kt + 1) * 128],
                        start=(kt == 0), stop=(kt == 7))),
                        reads=[("hnT", b), ("wb", kt)], writes=[("qk", ct)])
            for nt, pt, key in ((0, vps, "vps"), (1, gps, "gps")):
                for kt in range(8):
                    P.op("tensor", (lambda e, nt=nt, pt=pt, kt=kt, b=b: e.matmul(
                        pt[:, :], lhsT=hnT[b][:, kt * 128:(kt + 1) * 128], rhs=wb[:, kt, 512 + nt * 512:1024 + nt * 512],
                        start=(kt == 0), stop=(kt == 7))),
                        reads=[("hnT", b), ("wb", kt)], writes=[key])
            x1 = qk[:, 0::2, :]
            x2 = qk[:, 1::2, :]
            cb_ = cs[b][:, :].unsqueeze(1).to_broadcast([128, 2, 128])
            sb_ = sn[b][:, :].unsqueeze(1).to_broadcast([128, 2, 128])
            qkkeys = [("qk", i) for i in range(4)]
            P.op("vector", lambda e, x1=x1, cb_=cb_: e.tensor_tensor(out=t1[:, :, :], in0=x1, in1=cb_, op=ALU.mult),
                 reads=qkkeys + [("cs", b)], writes=["t1"])
            P.op("vector", lambda e, x2=x2, sb_=sb_: e.tensor_tensor(out=t2[:, :, :], in0=x2, in1=sb_, op=ALU.mult),
                 reads=qkkeys + [("sn", b)], writes=["t2"])
            P.op("vector", lambda e, b=b: e.tensor_tensor(out=qkr[b][:, 0::2, :], in0=t1[:, :, :], in1=t2[:, :, :], op=ALU.subtract),
                 reads=["t1", "t2"], writes=[("qkr", b, 0)])
            P.op("vector", lambda e, x2=x2, cb_=cb_: e.tensor_tensor(out=t1[:, :, :], in0=x2, in1=cb_, op=ALU.mult),
                 reads=qkkeys + [("cs", b)], writes=["t1"])
            P.op("vector", lambda e, x1=x1, sb_=sb_: e.tensor_tensor(out=t2[:, :, :], in0=x1, in1=sb_, op=ALU.mult),
                 reads=qkkeys + [("sn", b)], writes=["t2"])
            P.op("vector", lambda e, b=b: e.tensor_tensor(out=qkr[b][:, 1::2, :], in0=t1[:, :, :], in1=t2[:, :, :], op=ALU.add),
                 reads=["t1", "t2"], writes=[("qkr", b, 1)])
            rk = [("qkr", b, 0), ("qkr", b, 1)]
            gqb = gq[:, :].unsqueeze(1).to_broadcast([128, 2, 128])
            P.op("gpsimd", lambda e, b=b, gqb=gqb: e.tensor_tensor(out=qd[b][:, :, :], in0=qkr[b][:, 0:2, :], in1=gqb, op=ALU.mult),
                 reads=rk + ["gq"], writes=[("qd", b)])
            for half in range(2):
                P.op("tensor", (lambda e, half=half, b=b: e.matmul(
                    sc[:, :], lhsT=qkr[b][:, 2 + half, :], rhs=qkr[b][:, half, :], start=(half == 0), stop=(half == 1))),
                    reads=rk, writes=["sc"])
            P.op("vector", lambda e, b=b: e.tensor_tensor(out=smT[b][:, :], in0=sc[:, :], in1=intraT[:, :], op=ALU.mult),
                 reads=["sc", "intraT"], writes=[("smT", b)])
            for half in range(2):
                P.op("tensor", (lambda e, half=half, b=b: e.transpose(out=kT[:, half * 128:(half + 1) * 128],
                                                                     in_=qkr[b][:, 2 + half, :], identity=ident[:, :])),
                     reads=rk + ["ident"], writes=["kT"])
            P.op("vector", lambda e, b=b: e.tensor_scalar(out=kd[b][:, :], in0=kT[:, :], scalar1=kdec[:, 0:1], scalar2=None,
                                                          op0=ALU.mult),
                 reads=["kT", "kdec"], writes=[("kd", b)])
            P.op("scalar", lambda e, b=b: e.copy(out=vb[b][:, :], in_=vps[:, :]), reads=["vps"], writes=[("vb", b)])
            P.op("tensor", lambda e, b=b: e.matmul(ops_[:, :], lhsT=smT[b][:, :], rhs=vb[b][:, :], start=True, stop=False),
                 reads=[("smT", b), ("vb", b)], writes=["ops"])
            for half in range(2):
                P.op("tensor", (lambda e, half=half, b=b: e.matmul(
                    ops_[:, :], lhsT=qd[b][:, half, :], rhs=Sb[:, half, :], start=False, stop=(half == 1))),
                    reads=[("qd", b), f"Sb{half}"], writes=["ops"])
            for half in range(2):
                P.op("tensor", (lambda e, half=half, b=b: e.matmul(
                    dS[:, :], lhsT=kd[b][:, half * 128:(half + 1) * 128], rhs=vb[b][:, :], start=True, stop=True)),
                    reads=[("kd", b), ("vb", b)], writes=["dS"])
                P.op("vector", (lambda e, half=half: e.scalar_tensor_tensor(
                    out=S[:, half, :], in0=S[:, half, :], scalar=cdv[:, 0:1], in1=dS[:, :], op0=ALU.mult, op1=ALU.add)),
                    reads=["dS", f"S{half}", "cdv"], writes=[f"S{half}"])
                P.op("gpsimd", (lambda e, half=half: e.tensor_copy(out=Sb[:, half, :], in_=S[:, half, :])),
                     reads=[f"S{half}"], writes=[f"Sb{half}"])
            st = stat[b]
            P.op("scalar", lambda e, st=st: e.activation(out=junk[:, :], in_=ops_[:, :], func=AF.Identity, accum_out=st[:, 0:1]),
                 reads=["ops"], writes=["junk", ("stat", b, 0)])
            P.op("scalar", lambda e, st=st: e.activation(out=junk[:, :], in_=ops_[:, :], func=AF.Square, accum_out=st[:, 1:2]),
                 reads=["ops"], writes=["junk", ("stat", b, 1)])
            sk = [("stat", b, i) for i in range(6)]
            P.op("vector", lambda e, st=st: e.tensor_scalar(out=st[:, 2:4], in0=st[:, 0:2], scalar1=1.0 / 512, scalar2=None, op0=ALU.mult),
                 reads=sk[0:2], writes=sk[2:4])
            P.op("vector", lambda e, st=st: e.tensor_tensor(out=st[:, 4:5], in0=st[:, 2:3], in1=st[:, 2:3], op=ALU.mult),
                 reads=sk[2:3], writes=sk[4:5])
            P.op("vector", lambda e, st=st: e.tensor_tensor(out=st[:, 4:5], in0=st[:, 3:4], in1=st[:, 4:5], op=ALU.subtract),
                 reads=sk[3:5], writes=sk[4:5])
            P.op("vector", lambda e, st=st: e.tensor_scalar(out=st[:, 4:5], in0=st[:, 4:5], scalar1=RET_EPS, scalar2=None, op0=ALU.add),
                 reads=sk[4:5], writes=sk[4:5])
            P.op("scalar", lambda e, st=st: e.activation(out=st[:, 4:5], in_=st[:, 4:5], func=AF.Sqrt),
                 reads=sk[4:5], writes=sk[4:5])
            P.op("vector", lambda e, st=st: e.reciprocal(out=st[:, 4:5], in_=st[:, 4:5]), reads=sk[4:5], writes=sk[4:5])
            P.op("vector", lambda e, st=st: e.scalar_tensor_tensor(out=st[:, 5:6], in0=st[:, 2:3], scalar=-1.0, in1=st[:, 4:5],
                                                                   op0=ALU.mult, op1=ALU.mult),
                 reads=sk[2:5], writes=sk[5:6])
            P.op("scalar", lambda e, st=st, b=b: e.activation(out=on[b][:, :], in_=ops_[:, :], func=AF.Identity,
                                                              bias=st[:, 5:6], scale=st[:, 4:5]),
                 reads=["ops"] + sk[4:6], writes=[("on", b)])
            P.op("scalar", lambda e, b=b: e.activation(out=sg[b][:, :], in_=gps[:, :], func=AF.Silu),
                 reads=["gps"], writes=[("sg", b)])
            P.op("gpsimd", lambda e, b=b: e.tensor_tensor(out=on[b][:, :], in0=on[b][:, :], in1=gnw[:, :], op=ALU.mult),
                 reads=[("on", b), ("gnw",)], writes=[("on", b)])
            P.op("gpsimd", lambda e, b=b: e.tensor_tensor(out=on[b][:, :], in0=on[b][:, :], in1=gnb[:, :], op=ALU.add),
                 reads=[("on", b), ("gnb",)], writes=[("on", b)])
            P.op("vector", lambda e, b=b: e.tensor_tensor(out=yb[b][:, :], in0=on[b][:, :], in1=sg[b][:, :], op=ALU.mult),
                 reads=[("on", b), ("sg", b)], writes=[("yb", b)])
            P.dma("gpsimd", y_d[tsl, :], yb[b][:, :], reads=[("yb", b)])
        n = P.emit()
    return nc


def ssd_consts():
    i = np.arange(128)
    tri = (i[:, None] <= i[None, :]).astype(np.float32)
    return dict(tri=tri, ones=np.ones((128, 128), np.float32))


def build_ssd(L):
    nc = bass.Bass("TRN2", target_bir_lowering=False)
    x_d = nc.dram_tensor("x", [L, 1024], F32, kind="ExternalInput").ap()
    lnw_d = nc.dram_tensor("lnw", [1024], F32, kind="ExternalInput").ap()
    W = nc.dram_tensor("w", [1024, 516], F32, kind="ExternalInput").ap()
    cw_d = nc.dram_tensor("cw", [128, 16], F32, kind="ExternalInput").ap()
    cb_d = nc.dram_tensor("cb", [128, 4], F32, kind="ExternalInput").ap()
    dtb_d = nc.dram_tensor("dtb", [4], F32, kind="ExternalInput").ap()
    alog_d = nc.dram_tensor("alog", [4], F32, kind="ExternalInput").ap()
    dsk_d = nc.dram_tensor("dsk", [4], F32, kind="ExternalInput").ap()
    tri_d = nc.dram_tensor("tri", [128, 128], F32, kind="ExternalInput").ap()
    ones_d = nc.dram_tensor("ones", [128, 128], F32, kind="ExternalInput").ap()
    ident_d = nc.dram_tensor("ident_in", [128, 128], F32, kind="ExternalInput").ap()
    y_d = nc.dram_tensor("y", [L, 256], BF16, kind="ExternalOutput").ap()
    with ExitStack() as es:
        P = Prog(nc, es)
        identf = P.sb([128, 128], F32, name="identf")
        ident = P.sb([128, 128], BF16, name="ident")
        P.dma("sync", identf[:, :], ident_d[:, :], writes=["identf"])
        P.op("vector", lambda e: e.tensor_copy(out=ident[:, :], in_=identf[:, :]), reads=["identf"], writes=[("ident",)])
        tri = P.sb([128, 128], F32, name="tri")
        P.dma("sync", tri[:, :], tri_d[:, :], writes=["tri"])
        ones = P.sb([128, 128], F32, name="ones")
        P.dma("sync", ones[:, :], ones_d[:, :], writes=["ones"])
        cw = P.sb([128, 16], F32, name="cw")
        P.dma("sync", cw[:, :], cw_d[:, :], writes=["cw"])
        cb = P.sb([128, 4], F32, name="cb")
        P.dma("sync", cb[:, :], cb_d[:, :], writes=["cb"])
        lnw = bcast_row(P, lnw_d, 1024, "lnw")
        dtb = bcast_row(P, dtb_d, 4, "dtb")
        alog = bcast_row(P, alog_d, 4, "alog")
        dsk = bcast_row(P, dsk_d, 4, "dsk")
        aneg = P.sb([128, 4], F32, name="aneg")
        P.op("scalar", lambda e: e.activation(out=aneg[:, :], in_=alog[:, :], func=AF.Exp), reads=[("alog",)], writes=["aneg"])
        P.op("vector", lambda e: e.tensor_scalar(out=aneg[:, :], in0=aneg[:, :], scalar1=-1.0, scalar2=None, op0=ALU.mult),
             reads=["aneg"], writes=["aneg"])
        wb = load_weight_bf16(P, W, 1024, 516, "wb")
        st = alloc_norm_state(P, single_tp=True)
        pj = P.ps([128, 4, 128], F32, name="pj")
        arow = P.ps([128, 4, 128], F32, name="arow")
        bankA = P.ps([128, 512], F32, name="bankA")
        bankB = P.ps([128, 512], F32, name="bankB")
        bankC = P.ps([128, 512], F32, name="bankC")
        btp = P.ps([128, 128], BF16, name="btp")
        xtok_ps = bankA[:, 0:256]
        cb_ps = bankA[:, 256:384]
        sm_ps = bankA[:, 384:400]
        yd_ps = bankB[:, 0:256]
        yo_ps = bankB[:, 256:512]
        dst_ps = bankC[:, 0:256]
        xs = [P.sb([128, 1024], F32, name=f"xs{i}") for i in range(2)]
        xh = [P.sb([128, 4, 131], F32, name=f"xh{i}") for i in range(2)]
        cacc = [P.sb([128, 4, 128], F32, name=f"cacc{i}") for i in range(2)]
        xc = [P.sb([128, 2, 128], F32, name=f"xc{i}") for i in range(2)]
        bcT = [P.sb([128, 2, 128], BF16, name=f"bcT{i}") for i in range(2)]
        xtok = [P.sb([128, 256], F32, name=f"xtok{i}") for i in range(2)]
        btok = [P.sb([128, 128], BF16, name=f"btok{i}") for i in range(2)]
        sm = [P.sb([128, 32], F32, name=f"sm{i}") for i in range(2)]
        dabc = [P.sb([128, 4, 128], F32, name=f"dabc{i}") for i in range(2)]
        Dm = P.sb([128, 4, 128], F32, name="Dm")
        Em = P.sb([128, 4, 128], F32, name="Em")
        cbm = P.sb([128, 128], F32, name="cbm")
        mask01 = tri
        LT = [P.sb([128, 4, 128], BF16, name=f"LT{i}") for i in range(2)]
        xdt = P.sb([128, 256], F32, name="xdt")
        xdtb = [P.sb([128, 256], BF16, name=f"xdtb{i}") for i in range(2)]
        xdtd = [P.sb([128, 256], BF16, name=f"xdtd{i}") for i in range(2)]
        yos = P.sb([128, 256], F32, name="yos")
        ysk = P.sb([128, 256], F32, name="ysk")
        yout = [P.sb([128, 256], BF16, name=f"yout{i}") for i in range(2)]
        stT = P.sb([128, 256], F32, name="stT")
        prevb = P.sb([128, 256], BF16, name="prevb")
        P.op("vector", lambda e: e.memset(stT[:, :], 0.0), writes=["stT"])
        P.op("vector", lambda e: e.memset(prevb[:, :], 0.0), writes=["prevb"])
        P.op("vector", lambda e: e.memset(xh[0][:, :, 0:3], 0.0), writes=[("xhh", 0)])

        for c in range(L // 128):
            b = c % 2
            tsl = slice(c * 128, (c + 1) * 128)
            P.dma("sync", xs[b][:, :], x_d[tsl, :], writes=[("xs", b)])
            hnT, hkey = rms_norm_T(P, xs[b][:, :], ("xs", b), lnw, ("lnw",), ident, st, c)
            for ct in range(4):
                for kt in range(8):
                    P.op("tensor", (lambda e, ct=ct, kt=kt, hnT=hnT: e.matmul(
                        pj[:, ct, :], lhsT=wb[:, kt, ct * 128:(ct + 1) * 128], rhs=hnT[:, kt * 128:(kt + 1) * 128],
                        start=(kt == 0), stop=(kt == 7))), reads=[hkey, ("wb", kt)], writes=["pj"])
            for kt in range(8):
                P.op("tensor", (lambda e, kt=kt, hnT=hnT: e.matmul(
                    sm_ps[:, 0:4], lhsT=hnT[:, kt * 128:(kt + 1) * 128], rhs=wb[:, kt, 512:516],
                    start=(kt == 0), stop=(kt == 7))), reads=[hkey, ("wb", kt)], writes=["sm_ps"])
            P.op("scalar", lambda e, b=b: e.copy(out=xh[b][:, :, 3:131], in_=pj[:, :, :]), reads=["pj"], writes=[("xh", b)])
            P.op("gpsimd", lambda e, b=b: e.tensor_copy(out=xh[1 - b][:, :, 0:3], in_=xh[b][:, :, 128:131]),
                 reads=[("xh", b)], writes=[("xhh", 1 - b)])
            for ct in range(4):
                eng = "vector"
                P.op(eng, (lambda e, ct=ct, b=b: e.tensor_scalar(out=cacc[b][:, ct, :], in0=xh[b][:, ct, 0:128],
                                                                scalar1=cw[:, ct * 4:ct * 4 + 1], scalar2=None, op0=ALU.mult)),
                     reads=[("xh", b), ("xhh", b), "cw"], writes=[("cacc", b, ct)])
                for j in range(1, 4):
                    P.op(eng, (lambda e, ct=ct, b=b, j=j: e.scalar_tensor_tensor(
                        out=cacc[b][:, ct, :], in0=xh[b][:, ct, j:j + 128], scalar=cw[:, ct * 4 + j:ct * 4 + j + 1],
                        in1=cacc[b][:, ct, :], op0=ALU.mult, op1=ALU.add)),
                        reads=[("xh", b), ("xhh", b), "cw", ("cacc", b, ct)], writes=[("cacc", b, ct)])
                if ct < 2:
                    P.op("scalar", (lambda e, ct=ct, b=b: e.activation(out=xc[b][:, ct, :], in_=cacc[b][:, ct, :], func=AF.Silu,
                                                                     bias=cb[:, ct:ct + 1])),
                         reads=[("cacc", b, ct), "cb"], writes=[("xc", b, ct)])
                else:
                    P.op("scalar", (lambda e, ct=ct, b=b: e.activation(out=bcT[b][:, ct - 2, :], in_=cacc[b][:, ct, :], func=AF.Silu,
                                                                     bias=cb[:, ct:ct + 1])),
                         reads=[("cacc", b, ct), "cb"], writes=[("bcT", b, ct - 2)])
            for h2 in range(2):
                P.op("tensor", (lambda e, h2=h2, b=b: e.transpose(out=xtok_ps[:, h2 * 128:(h2 + 1) * 128], in_=xc[b][:, h2, :],
                                                                 identity=identf[:, :])),
                     reads=[("xc", b, h2), "identf"], writes=["xtok_ps"])
            P.op("tensor", lambda e, b=b: e.transpose(out=btp[:, :], in_=bcT[b][:, 0, :], identity=ident[:, :]),
                 reads=[("bcT", b, 0), ("ident",)], writes=["btp"])
            P.op("scalar", lambda e, b=b: e.copy(out=xtok[b][:, :], in_=xtok_ps), reads=["xtok_ps"], writes=[("xtok", b)])
            P.op("scalar", lambda e, b=b: e.copy(out=btok[b][:, :], in_=btp[:, :]), reads=["btp"], writes=[("btok", b)])
            s_ = sm[b]
            P.op("vector", lambda e, s_=s_: e.tensor_tensor(out=s_[:, 0:4], in0=sm_ps[:, 0:4], in1=dtb[:, :], op=ALU.add),
                 reads=["sm_ps", ("dtb",)], writes=[("sm", b, "dt")])
            P.op("scalar", lambda e, s_=s_: e.activation(out=s_[:, 0:4], in_=s_[:, 0:4], func=AF.Exp),
                 reads=[("sm", b, "dt")], writes=[("sm", b, "dt")])
            P.op("scalar", lambda e, s_=s_: e.activation(out=s_[:, 0:4], in_=s_[:, 0:4], func=AF.Ln, bias=1.0),
                 reads=[("sm", b, "dt")], writes=[("sm", b, "dt")])
            P.op("vector", lambda e, s_=s_: e.tensor_tensor(out=s_[:, 4:8], in0=s_[:, 0:4], in1=aneg[:, :], op=ALU.mult),
                 reads=[("sm", b, "dt"), "aneg"], writes=[("sm", b, "da")])
            P.op("vector", lambda e, s_=s_, b=b: e.tensor_copy(out=dabc[b][:, :, :],
                                                              in_=s_[:, 4:8].unsqueeze(2).to_broadcast([128, 4, 128])),
                 reads=[("sm", b, "da")], writes=[("dabc", b)])
            P.op("tensor", lambda e, s_=s_: e.matmul(sm_ps[:, 4:8], lhsT=tri[:, :], rhs=s_[:, 4:8], start=True, stop=True),
                 reads=["tri", ("sm", b, "da")], writes=["sm_ps2"])
            P.op("tensor", lambda e, s_=s_: e.matmul(sm_ps[:, 8:12], lhsT=ones[:, :], rhs=s_[:, 4:8], start=True, stop=True),
                 reads=["ones", ("sm", b, "da")], writes=["sm_ps2"])
            for h in range(4):
                P.op("tensor", (lambda e, h=h, b=b: e.matmul(arow[:, h, :], lhsT=dabc[b][:, h, :], rhs=tri[:, :], start=True, stop=True)),
                     reads=[("dabc", b), "tri"], writes=["arow"])
            P.op("vector", lambda e, s_=s_: e.tensor_copy(out=s_[:, 8:16], in_=sm_ps[:, 4:12]), reads=["sm_ps2"], writes=[("sm", b, "acs")])
            for h in range(4):
                P.op("vector", (lambda e, h=h, s_=s_: e.tensor_scalar(out=Dm[:, h, :], in0=arow[:, h, :], scalar1=s_[:, 8 + h:9 + h],
                                                                    scalar2=0.0, op0=ALU.subtract, op1=ALU.min)),
                     reads=["arow", ("sm", b, "acs")], writes=["Dm"])
            P.op("scalar", lambda e: e.activation(out=Em[:, :, :], in_=Dm[:, :, :], func=AF.Exp), reads=["Dm"], writes=["Em"])
            P.op("scalar", lambda e, s_=s_: e.activation(out=s_[:, 16:20], in_=s_[:, 8:12], func=AF.Exp),
                 reads=[("sm", b, "acs")], writes=[("sm", b, "ea")])
            P.op("vector", lambda e, s_=s_: e.tensor_tensor(out=s_[:, 20:24], in0=s_[:, 12:16], in1=s_[:, 8:12], op=ALU.subtract),
                 reads=[("sm", b, "acs")], writes=[("sm", b, "dsts")])
            P.op("scalar", lambda e, s_=s_: e.activation(out=s_[:, 20:24], in_=s_[:, 20:24], func=AF.Exp),
                 reads=[("sm", b, "dsts")], writes=[("sm", b, "dsts")])
            P.op("scalar", lambda e, s_=s_: e.activation(out=s_[:, 24:28], in_=s_[:, 12:16], func=AF.Exp),
                 reads=[("sm", b, "acs")], writes=[("sm", b, "cdv")])
            P.op("tensor", lambda e, b=b: e.matmul(cb_ps, lhsT=bcT[b][:, 0, :], rhs=bcT[b][:, 1, :], start=True, stop=True),
                 reads=[("bcT", b, 0), ("bcT", b, 1)], writes=["cb_ps"])
            P.op("vector", lambda e: e.tensor_tensor(out=cbm[:, :], in0=cb_ps, in1=mask01[:, :], op=ALU.mult),
                 reads=["cb_ps", "tri"], writes=["cbm"])
            P.op("gpsimd", lambda e, b=b: e.tensor_tensor(out=LT[b][:, :, :], in0=Em[:, :, :],
                                                         in1=cbm[:, :].unsqueeze(1).to_broadcast([128, 4, 128]), op=ALU.mult),
                 reads=["Em", "cbm"], writes=[("LT", b)])
            x3 = xtok[b][:, :].rearrange("p (h d) -> p h d", h=4)
            P.op("vector", lambda e, s_=s_, x3=x3: e.tensor_tensor(out=xdt[:, :].rearrange("p (h d) -> p h d", h=4), in0=x3,
                                                                  in1=s_[:, 0:4].unsqueeze(2).to_broadcast([128, 4, 64]), op=ALU.mult),
                 reads=[("xtok", b), ("sm", b, "dt")], writes=["xdt"])
            P.op("gpsimd", lambda e, b=b: e.tensor_copy(out=xdtb[b][:, :], in_=xdt[:, :]), reads=["xdt"], writes=[("xdtb", b)])
            P.op("vector", lambda e, s_=s_, b=b: e.tensor_tensor(out=xdtd[b][:, :].rearrange("p (h d) -> p h d", h=4),
                                                                in0=xdt[:, :].rearrange("p (h d) -> p h d", h=4),
                                                                in1=s_[:, 20:24].unsqueeze(2).to_broadcast([128, 4, 64]), op=ALU.mult),
                 reads=["xdt", ("sm", b, "dsts")], writes=[("xdtd", b)])
            for h in range(4):
                P.op("tensor", (lambda e, h=h, b=b: e.matmul(yd_ps[:, h * 64:(h + 1) * 64], lhsT=LT[b][:, h, :],
                                                            rhs=xdtb[b][:, h * 64:(h + 1) * 64], start=True, stop=True)),
                     reads=[("LT", b), ("xdtb", b)], writes=["yd_ps"])
            P.op("tensor", lambda e, b=b: e.matmul(yo_ps, lhsT=bcT[b][:, 1, :], rhs=prevb[:, :], start=True, stop=True),
                 reads=[("bcT", b, 1), "prevb"], writes=["yo_ps"])
            P.op("vector", lambda e, s_=s_: e.tensor_tensor(out=yos[:, :].rearrange("p (h d) -> p h d", h=4),
                                                           in0=yo_ps.rearrange("p (h d) -> p h d", h=4),
                                                           in1=s_[:, 16:20].unsqueeze(2).to_broadcast([128, 4, 64]), op=ALU.mult),
                 reads=["yo_ps", ("sm", b, "ea")], writes=["yos"])
            P.op("gpsimd", lambda e, x3=x3: e.tensor_tensor(out=ysk[:, :].rearrange("p (h d) -> p h d", h=4), in0=x3,
                                                           in1=dsk[:, :].unsqueeze(2).to_broadcast([128, 4, 64]), op=ALU.mult),
                 reads=[("xtok", b), ("dsk",)], writes=["ysk"])
            P.op("vector", lambda e: e.tensor_tensor(out=yos[:, :], in0=yd_ps, in1=yos[:, :], op=ALU.add),
                 reads=["yd_ps", "yos"], writes=["yos"])
            P.op("gpsimd", lambda e, b=b: e.tensor_tensor(out=yout[b][:, :], in0=yos[:, :], in1=ysk[:, :], op=ALU.add),
                 reads=["yos", "ysk"], writes=[("yout", b)])
            P.dma("gpsimd", y_d[tsl, :], yout[b][:, :], reads=[("yout", b)])
            P.op("tensor", lambda e, b=b: e.matmul(dst_ps, lhsT=btok[b][:, :], rhs=xdtd[b][:, :], start=True, stop=True),
                 reads=[("btok", b), ("xdtd", b)], writes=["dst_ps"])
            P.op("vector", lambda e, s_=s_: e.tensor_tensor(out=stT[:, :].rearrange("p (h d) -> p h d", h=4),
                                                           in0=stT[:, :].rearrange("p (h d) -> p h d", h=4),
                                                           in1=s_[:, 24:28].unsqueeze(2).to_broadcast([128, 4, 64]), op=ALU.mult),
                 reads=["stT", ("sm", b, "cdv")], writes=["stT"])
            P.op("vector", lambda e: e.tensor_tensor(out=stT[:, :], in0=stT[:, :], in1=dst_ps, op=ALU.add),
                 reads=["stT", "dst_ps"], writes=["stT"])
            P.op("scalar", lambda e: e.copy(out=prevb[:, :], in_=stT[:, :]), reads=["stT"], writes=["prevb"])
        P.emit()
    return nc


TWO_PI = 2.0 * math.pi
NPRM = 70
GELU_FUNC = AF.Gelu_apprx_tanh


def s5_consts():
    i = np.arange(128)
    c = {}
    c["sgnA"] = np.concatenate([-np.ones(64), np.ones(64)]).astype(np.float32)[:, None]
    c["sgnB"] = np.concatenate([np.ones(64), -np.ones(64)]).astype(np.float32)[:, None]
    c["ktB"] = np.tile((127.0 - i)[None, :], (128, 1)).astype(np.float32)
    c["ktN"] = np.tile((-np.arange(8.0))[None, :], (128, 1)).astype(np.float32)
    c["ktC"] = np.tile(np.arange(129.0)[None, :], (128, 1)).astype(np.float32)
    c["ktc"] = np.tile(np.arange(128.0)[None, :], (128, 1)).astype(np.float32)
    sl = i // 16
    c["mask0"] = (sl[None, :] >= sl[:, None]).astype(np.float32)
    return c


def s5_pack_params(lam_re, lam_im, log_dt, b_re, b_im, c_re, c_im, dskip, groups):
    out = np.zeros((128, len(groups), NPRM), np.float32)
    for n, g in enumerate(groups):
        out[:, n, 0] = np.concatenate([lam_re[g], lam_re[g]])
        out[:, n, 1] = np.concatenate([lam_im[g], lam_im[g]])
        out[:, n, 2] = log_dt[g]
        out[:, n, 3:19] = np.concatenate([b_re[g], b_im[g]], 0)
        out[:, n, 19:35] = np.concatenate([b_im[g], b_re[g]], 0)
        out[:, n, 35:51] = np.concatenate([c_re[g].T, c_im[g].T], 0)
        out[:, n, 51:67] = np.concatenate([c_im[g].T, c_re[g].T], 0)
        out[:, n, 67] = np.tile(dskip[16 * g:16 * g + 16], 8)
    return out


def build_s5(L):
    NCH = L // 128
    NJ = L // 8
    NSC = L // 1024
    CT = min(32, NCH)
    NNT = NCH // CT
    nc = bass.Bass("TRN2", target_bir_lowering=False)
    x_d = nc.dram_tensor("x", [L, 1024], F32, kind="ExternalInput").ap()
    lnw_d = nc.dram_tensor("lnw", [1024], F32, kind="ExternalInput").ap()
    W = nc.dram_tensor("w", [1024, 256], F32, kind="ExternalInput").ap()
    prm_d = nc.dram_tensor("prm", [128, 16, NPRM], F32, kind="ExternalInput").ap()
    cd = {}
    for nm, shp in (("sgnA", [128, 1]), ("sgnB", [128, 1]), ("ktB", [128, 128]), ("ktN", [128, 8]), ("ktC", [128, 129]),
                    ("ktc", [128, 128]), ("mask0", [128, 128]), ("ident_in", [128, 128])):
        cd[nm] = nc.dram_tensor(nm, shp, F32, kind="ExternalInput").ap()
    y_d = nc.dram_tensor("y", [L, 256], BF16, kind="ExternalOutput").ap()
    with ExitStack() as es:
        P = Prog(nc, es)
        cs = {}
        for nm in cd:
            shp = cd[nm].shape
            cs[nm] = P.sb(list(shp), F32, name="c_" + nm)
            P.dma("sync", cs[nm][:, :], cd[nm][:, :], writes=[nm])
        identf = cs["ident_in"]
        ident = P.sb([128, 128], BF16, name="ident")
        P.op("vector", lambda e: e.tensor_copy(out=ident[:, :], in_=identf[:, :]), reads=["ident_in"], writes=["ident"])
        prm = P.sb([128, 16, NPRM], F32, name="prm")
        P.dma("sync", prm[:, :, :], prm_d[:, :, :], writes=["prm"])
        lnw = bcast_row(P, lnw_d, 1024, "lnw")
        wb = load_weight_bf16(P, W, 1024, 256, "wb")
        U = P.sb([128, 16, NJ], BF16, name="U")
        tp = P.ps([128, 1024], BF16, name="tp")
        big = [P.ps([128, 512], F32, name=f"big{i}") for i in range(4)]
        bf2 = [P.ps([128, 1024], BF16, name=f"bf2_{i}") for i in range(2)]
        lcb = P.ps([128, 512], F32, name="lcb")
        xs = [P.sb([128, 1024], F32, name=f"xs{i}") for i in range(2)]
        junk = P.sb([128, 1024], BF16, name="junk")
        ss = [P.sb([128, 2], F32, name=f"ss{i}") for i in range(2)]
        hn = [P.sb([128, 1024], BF16, name=f"hn{i}") for i in range(2)]
        hsc = [P.sb([128, 8, 1024], BF16, name="hsc0")]
        uS = [P.sb([128, 16, 8, 16], BF16, name=f"uS{i}") for i in range(2)]
        for sc in range(NSC):
            sb_ = sc % 2
            for cc in range(8):
                c = sc * 8 + cc
                b = c % 2
                P.dma("sync", xs[b][:, :], x_d[c * 128:(c + 1) * 128, :], writes=[("xs", b)])
                P.op("scalar", lambda e, b=b: e.activation(out=junk[:, :], in_=xs[b][:, :], func=AF.Square, accum_out=ss[b][:, 0:1]),
                     reads=[("xs", b)], writes=["junk", ("ss", b)])
                P.op("vector", lambda e, b=b: e.tensor_scalar(out=ss[b][:, 1:2], in0=ss[b][:, 0:1], scalar1=1.0 / 1024, scalar2=EPS,
                                                              op0=ALU.mult, op1=ALU.add), reads=[("ss", b)], writes=[("rs", b)])
                P.op("scalar", lambda e, b=b: e.activation(out=ss[b][:, 1:2], in_=ss[b][:, 1:2], func=AF.Sqrt),
                     reads=[("rs", b)], writes=[("rs", b)])
                P.op("vector", lambda e, b=b: e.reciprocal(out=ss[b][:, 1:2], in_=ss[b][:, 1:2]), reads=[("rs", b)], writes=[("rs", b)])
                P.op("vector", lambda e, b=b: e.scalar_tensor_tensor(out=hn[b][:, :], in0=xs[b][:, :], scalar=ss[b][:, 1:2],
                                                                     in1=lnw[:, :], op0=ALU.mult, op1=ALU.mult),
                     reads=[("xs", b), ("rs", b), ("lnw",)], writes=[("hn", b)])
                for kt in range(8):
                    P.op("tensor", (lambda e, kt=kt, b=b: e.transpose(out=tp[:, kt * 128:(kt + 1) * 128],
                                                                     in_=hn[b][:, kt * 128:(kt + 1) * 128], identity=ident[:, :])),
                         reads=[("hn", b), "ident"], writes=["tp"])
                P.op("scalar", lambda e, sb_=sb_, cc=cc: e.copy(out=hsc[0][:, :, cc * 128:(cc + 1) * 128],
                                                               in_=tp[:, :].rearrange("p (k t) -> p k t", k=8)),
                     reads=["tp"], writes=[("hsc", 0, cc)])
            hk = [("hsc", 0, cc) for cc in range(8)]
            for s in range(8):
                pt = big[s // 2]
                for kt in range(8):
                    P.op("tensor", (lambda e, s=s, kt=kt, pt=pt, sb_=sb_: e.matmul(
                        pt[:, (s % 2) * 256:(s % 2 + 1) * 256], lhsT=hsc[0][:, kt, s::8], rhs=wb[:, kt, :],
                        start=(kt == 0), stop=(kt == 7))), reads=hk + [("wb", kt)], writes=[("big", s // 2)])
            for q in range(4):
                eng = "scalar" if q % 2 == 0 else "vector"
                if eng == "scalar":
                    P.op(eng, lambda e, q=q, sb_=sb_: e.copy(out=uS[sb_][:, :, 2 * q:2 * q + 2, :].rearrange("p g s h -> p s g h"),
                                                            in_=big[q][:, :].rearrange("p (s g h) -> p s g h", s=2, g=16)),
                         reads=[("big", q)], writes=[("uS", sb_, q)])
                else:
                    P.op(eng, lambda e, q=q, sb_=sb_: e.tensor_copy(out=uS[sb_][:, :, 2 * q:2 * q + 2, :].rearrange("p g s h -> p s g h"),
                                                                   in_=big[q][:, :].rearrange("p (s g h) -> p s g h", s=2, g=16)),
                         reads=[("big", q)], writes=[("uS", sb_, q)])
            uk = [("uS", sb_, q) for q in range(4)]
            for g in range(16):
                P.op("tensor", (lambda e, g=g, sb_=sb_: e.transpose(out=bf2[g // 8][:, (g % 8) * 128:(g % 8 + 1) * 128],
                                                                   in_=uS[sb_][:, g, :, :].rearrange("p s h -> p (s h)"), identity=ident[:, :])),
                     reads=uk + ["ident"], writes=[("bf2", g // 8)])
            for hf in range(2):
                eng = "scalar" if hf == 0 else "vector"
                fn = (lambda e, hf=hf, sc=sc: e.copy(out=U[:, 8 * hf:8 * hf + 8, sc * 128:(sc + 1) * 128],
                                                     in_=bf2[hf][:, :].rearrange("p (g j) -> p g j", g=8))) if hf == 0 else \
                     (lambda e, hf=hf, sc=sc: e.tensor_copy(out=U[:, 8 * hf:8 * hf + 8, sc * 128:(sc + 1) * 128],
                                                            in_=bf2[hf][:, :].rearrange("p (g j) -> p g j", g=8)))
                P.op(eng, fn, reads=[("bf2", hf)], writes=[("U", g_, sc * 128 // (CT * 16)) for g_ in range(8 * hf, 8 * hf + 8)])

        sm = P.sb([128, 48], F32, name="sm")
        smi = P.sb([128, 4], mybir.dt.int32, name="smi")
        PA = P.sb([128, 2, 256], F32, name="PA")
        PN = P.sb([128, 2, 256], F32, name="PN")
        PW = P.sb([128, 2, 256], F32, name="PW")
        pw = P.sb([128, 8], F32, name="pw")
        bb = P.sb([128, 12, 16], F32, name="bb")
        tmpA = {e_: P.sb([128, 65 * 16], F32, name=f"tmpA{e_}") for e_ in ("vector", "gpsimd")}
        tmpB = {e_: P.sb([128, 65 * 16], F32, name=f"tmpB{e_}") for e_ in ("vector", "gpsimd")}
        TBr = P.sb([128, 128, 16], BF16, name="TBr")
        TBrs = P.sb([128, 128, 16], BF16, name="TBrs")
        TBn = P.sb([128, 8, 16], BF16, name="TBn")
        TC = P.sb([128, 129, 16], BF16, name="TC")
        Wst = P.sb([128, 16, 128], BF16, name="Wst")
        Wsts = P.sb([128, 16, 128], BF16, name="Wsts")
        Kt = P.sb([128, 16, 128], BF16, name="Kt")
        r128t = P.sb([128, 128], F32, name="r128t")
        sgs = P.sb([128, 2, 128], F32, name="sgs")
        rot = P.sb([128, 6, NCH], F32, name="rot")
        lcs = P.sb([128, 2, NCH], F32, name="lcs")
        xcin = P.sb([128, NCH], BF16, name="xcin")
        ytmp = [P.sb([128, CT * 16], F32, name=f"ytmp{i}") for i in range(2)]
        sgnA, sgnB = cs["sgnA"], cs["sgnB"]
        C1 = 6.28125
        C2 = TWO_PI - 6.28125

        def V(eng, fn, reads, writes):
            P.op(eng, fn, reads=reads, writes=writes)

        def S(fn):
            P.op("vector", fn, reads=["sm", "prm", "PA"], writes=["sm"])

        def reduce_angle(src, dst, tcol):
            S(lambda e: e.tensor_scalar(out=sm[:, tcol:tcol + 1], in0=sm[:, src:src + 1], scalar1=1.0 / TWO_PI, scalar2=None, op0=ALU.mult))
            P.op("vector", lambda e: e.tensor_copy(out=smi[:, 0:1], in_=sm[:, tcol:tcol + 1]), reads=["sm"], writes=["smi"])
            P.op("vector", lambda e: e.tensor_copy(out=sm[:, tcol:tcol + 1], in_=smi[:, 0:1]), reads=["smi", "sm"], writes=["sm"])
            S(lambda e: e.scalar_tensor_tensor(out=sm[:, dst:dst + 1], in0=sm[:, tcol:tcol + 1], scalar=-C1, in1=sm[:, src:src + 1],
                                               op0=ALU.mult, op1=ALU.add))
            S(lambda e: e.scalar_tensor_tensor(out=sm[:, dst:dst + 1], in0=sm[:, tcol:tcol + 1], scalar=-C2, in1=sm[:, dst:dst + 1],
                                               op0=ALU.mult, op1=ALU.add))
            S(lambda e: e.tensor_scalar(out=sm[:, tcol:tcol + 1], in0=sm[:, dst:dst + 1], scalar1=math.pi, scalar2=TWO_PI,
                                        op0=ALU.is_gt, op1=ALU.mult))
            S(lambda e: e.tensor_tensor(out=sm[:, dst:dst + 1], in0=sm[:, dst:dst + 1], in1=sm[:, tcol:tcol + 1], op=ALU.subtract))
            S(lambda e: e.tensor_scalar(out=sm[:, tcol:tcol + 1], in0=sm[:, dst:dst + 1], scalar1=-math.pi, scalar2=TWO_PI,
                                        op0=ALU.is_lt, op1=ALU.mult))
            S(lambda e: e.tensor_tensor(out=sm[:, dst:dst + 1], in0=sm[:, dst:dst + 1], in1=sm[:, tcol:tcol + 1], op=ALU.add))

        def powers(T, tkey, rcol, icol, nsteps):
            V("vector", lambda e: e.memset(T[:, 0, 0:1], 1.0), [], [tkey])
            V("vector", lambda e: e.memset(T[:, 1, 0:1], 0.0), [tkey], [tkey])
            V("vector", lambda e: e.tensor_copy(out=pw[:, 0:1], in_=sm[:, rcol:rcol + 1]), ["sm", "pw"], ["pw"])
            V("vector", lambda e: e.tensor_copy(out=pw[:, 1:2], in_=sm[:, icol:icol + 1]), ["sm", "pw"], ["pw"])
            for i in range(nsteps):
                n = 1 << i
                pr_, pi_ = pw[:, 0:1], pw[:, 1:2]
                V("vector", lambda e, n=n, pi_=pi_: e.tensor_scalar(out=T[:, 0, n:2 * n], in0=T[:, 1, 0:n], scalar1=pi_, scalar2=None, op0=ALU.mult),
                  [tkey, "pw"], [tkey])
                V("vector", lambda e, n=n, pr_=pr_: e.scalar_tensor_tensor(out=T[:, 0, n:2 * n], in0=T[:, 0, 0:n], scalar=pr_, in1=T[:, 0, n:2 * n],
                                                                       op0=ALU.mult, op1=ALU.subtract), [tkey, "pw"], [tkey])
                V("vector", lambda e, n=n, pr_=pr_: e.tensor_scalar(out=T[:, 1, n:2 * n], in0=T[:, 1, 0:n], scalar1=pr_, scalar2=None, op0=ALU.mult),
                  [tkey, "pw"], [tkey])
                V("vector", lambda e, n=n, pi_=pi_: e.scalar_tensor_tensor(out=T[:, 1, n:2 * n], in0=T[:, 0, 0:n], scalar=pi_, in1=T[:, 1, n:2 * n],
                                                                       op0=ALU.mult, op1=ALU.add), [tkey, "pw"], [tkey])
                if i < nsteps - 1:
                    V("vector", lambda e: e.tensor_tensor(out=pw[:, 2:3], in0=pw[:, 1:2], in1=pw[:, 1:2], op=ALU.mult), ["pw"], ["pw"])
                    V("vector", lambda e: e.tensor_tensor(out=pw[:, 3:4], in0=pw[:, 0:1], in1=pw[:, 1:2], op=ALU.mult), ["pw"], ["pw"])
                    V("vector", lambda e: e.scalar_tensor_tensor(out=pw[:, 0:1], in0=pw[:, 0:1], scalar=pw[:, 0:1], in1=pw[:, 2:3],
                                                                 op0=ALU.mult, op1=ALU.subtract), ["pw"], ["pw"])
                    V("vector", lambda e: e.tensor_scalar(out=pw[:, 1:2], in0=pw[:, 3:4], scalar1=2.0, scalar2=None, op0=ALU.mult), ["pw"], ["pw"])

        def table(out3, outkey, T, tkey, n, i1, i2, eng):
            for k0 in range(0, n, 65):
                m = min(65, n - k0)
                a3 = tmpA[eng][:, 0:m * 16].rearrange("p (k h) -> p k h", h=16)
                b3 = tmpB[eng][:, 0:m * 16].rearrange("p (k h) -> p k h", h=16)
                V(eng, lambda e, a3=a3, k0=k0, m=m: e.tensor_tensor(out=a3, in0=T[:, 0, k0:k0 + m].unsqueeze(2).to_broadcast([128, m, 16]),
                                                                 in1=bb[:, i1, :].unsqueeze(1).to_broadcast([128, m, 16]), op=ALU.mult),
                  [tkey, ("bb", i1)], ["tmpA" + eng])
                V(eng, lambda e, b3=b3, k0=k0, m=m: e.tensor_tensor(out=b3, in0=T[:, 1, k0:k0 + m].unsqueeze(2).to_broadcast([128, m, 16]),
                                                                 in1=bb[:, i2, :].unsqueeze(1).to_broadcast([128, m, 16]), op=ALU.mult),
                  [tkey, ("bb", i2)], ["tmpB" + eng])
                V(eng, lambda e, a3=a3, b3=b3, k0=k0, m=m: e.tensor_tensor(out=out3[:, k0:k0 + m, :], in0=a3, in1=b3, op=ALU.add),
                  ["tmpA" + eng, "tmpB" + eng], [outkey])

        def cmul_small(dst1, dst2, src1, src2, rc, ic):
            k = lambda i: ("bb", i)
            V("vector", lambda e: e.tensor_scalar(out=bb[:, dst1, :], in0=bb[:, src1, :], scalar1=sm[:, rc:rc + 1], scalar2=None, op0=ALU.mult),
              [k(src1), "sm"], [k(dst1)])
            V("vector", lambda e: e.scalar_tensor_tensor(out=bb[:, dst1, :], in0=bb[:, src2, :], scalar=sm[:, ic:ic + 1], in1=bb[:, dst1, :],
                                                         op0=ALU.mult, op1=ALU.add), [k(src2), k(dst1), "sm"], [k(dst1)])
            V("vector", lambda e: e.tensor_scalar(out=bb[:, dst2, :], in0=bb[:, src1, :], scalar1=sm[:, ic:ic + 1], scalar2=-1.0,
                                                  op0=ALU.mult, op1=ALU.mult), [k(src1), "sm"], [k(dst2)])
            V("vector", lambda e: e.scalar_tensor_tensor(out=bb[:, dst2, :], in0=bb[:, src2, :], scalar=sm[:, rc:rc + 1], in1=bb[:, dst2, :],
                                                         op0=ALU.mult, op1=ALU.add), [k(src2), k(dst2), "sm"], [k(dst2)])

        ny = 0
        for g in range(16):
            pg = prm[:, g, :]
            V("scalar", lambda e, pg=pg: e.activation(out=sm[:, 0:1], in_=pg[:, 2:3], func=AF.Exp), ["prm", "sm"], ["sm"])
            S(lambda e, pg=pg: e.tensor_scalar(out=sm[:, 1:2], in0=pg[:, 0:1], scalar1=-1e-4, scalar2=None, op0=ALU.min))
            S(lambda e: e.tensor_tensor(out=sm[:, 2:3], in0=sm[:, 1:2], in1=sm[:, 0:1], op=ALU.mult))
            S(lambda e, pg=pg: e.tensor_tensor(out=sm[:, 3:4], in0=pg[:, 1:2], in1=sm[:, 0:1], op=ALU.mult))
            V("scalar", lambda e: e.activation(out=sm[:, 13:14], in_=sm[:, 2:3], func=AF.Exp), ["sm"], ["sm"])
            reduce_angle(3, 14, 30)
            S(lambda e: e.tensor_scalar(out=sm[:, 15:16], in0=sm[:, 3:4], scalar1=0.5 * math.pi, scalar2=None, op0=ALU.add))
            reduce_angle(15, 16, 32)
            V("scalar", lambda e: e.activation(out=sm[:, 17:18], in_=sm[:, 14:15], func=AF.Sin), ["sm"], ["sm"])
            V("scalar", lambda e: e.activation(out=sm[:, 18:19], in_=sm[:, 16:17], func=AF.Sin), ["sm"], ["sm"])
            S(lambda e: e.tensor_tensor(out=sm[:, 4:5], in0=sm[:, 13:14], in1=sm[:, 18:19], op=ALU.mult))
            S(lambda e: e.tensor_tensor(out=sm[:, 5:6], in0=sm[:, 13:14], in1=sm[:, 17:18], op=ALU.mult))
            S(lambda e: e.tensor_tensor(out=sm[:, 19:20], in0=sm[:, 4:5], in1=sm[:, 4:5], op=ALU.mult))
            S(lambda e: e.scalar_tensor_tensor(out=sm[:, 19:20], in0=sm[:, 5:6], scalar=sm[:, 5:6], in1=sm[:, 19:20], op0=ALU.mult, op1=ALU.add))
            S(lambda e: e.reciprocal(out=sm[:, 19:20], in_=sm[:, 19:20]))
            S(lambda e: e.tensor_tensor(out=sm[:, 20:21], in0=sm[:, 4:5], in1=sm[:, 19:20], op=ALU.mult))
            S(lambda e: e.scalar_tensor_tensor(out=sm[:, 21:22], in0=sm[:, 5:6], scalar=-1.0, in1=sm[:, 19:20], op0=ALU.mult, op1=ALU.mult))
            S(lambda e: e.tensor_tensor(out=sm[:, 6:7], in0=sm[:, 1:2], in1=sm[:, 1:2], op=ALU.mult))
            S(lambda e, pg=pg: e.scalar_tensor_tensor(out=sm[:, 6:7], in0=pg[:, 1:2], scalar=pg[:, 1:2], in1=sm[:, 6:7], op0=ALU.mult, op1=ALU.add))
            S(lambda e: e.reciprocal(out=sm[:, 6:7], in_=sm[:, 6:7]))
            S(lambda e: e.tensor_scalar(out=sm[:, 7:8], in0=sm[:, 4:5], scalar1=-1.0, scalar2=None, op0=ALU.add))
            S(lambda e: e.tensor_tensor(out=sm[:, 8:9], in0=sm[:, 7:8], in1=sm[:, 1:2], op=ALU.mult))
            S(lambda e, pg=pg: e.scalar_tensor_tensor(out=sm[:, 8:9], in0=sm[:, 5:6], scalar=pg[:, 1:2], in1=sm[:, 8:9], op0=ALU.mult, op1=ALU.add))
            S(lambda e: e.tensor_tensor(out=sm[:, 8:9], in0=sm[:, 8:9], in1=sm[:, 6:7], op=ALU.mult))
            S(lambda e: e.tensor_tensor(out=sm[:, 9:10], in0=sm[:, 5:6], in1=sm[:, 1:2], op=ALU.mult))
            S(lambda e, pg=pg: e.tensor_tensor(out=sm[:, 10:11], in0=sm[:, 7:8], in1=pg[:, 1:2], op=ALU.mult))
            S(lambda e: e.tensor_tensor(out=sm[:, 9:10], in0=sm[:, 9:10], in1=sm[:, 10:11], op=ALU.subtract))
            S(lambda e: e.tensor_tensor(out=sm[:, 9:10], in0=sm[:, 9:10], in1=sm[:, 6:7], op=ALU.mult))
            powers(PA, "PA", 4, 5, 8)
            powers(PN, "PN", 20, 21, 7)
            S(lambda e: e.tensor_copy(out=sm[:, 22:23], in_=PA[:, 0, 127:128]))
            S(lambda e: e.tensor_copy(out=sm[:, 23:24], in_=PA[:, 1, 127:128]))
            S(lambda e: e.tensor_tensor(out=sm[:, 24:25], in0=PA[:, 0, 128:129], in1=PA[:, 0, 128:129], op=ALU.mult))
            S(lambda e: e.scalar_tensor_tensor(out=sm[:, 24:25], in0=PA[:, 1, 128:129], scalar=PA[:, 1, 128:129], in1=sm[:, 24:25],
                                               op0=ALU.mult, op1=ALU.add))
            V("scalar", lambda e: e.activation(out=sm[:, 25:26], in_=sm[:, 24:25], func=AF.Sqrt), ["sm"], ["sm"])
            S(lambda e: e.reciprocal(out=sm[:, 34:35], in_=sm[:, 25:26]))
            S(lambda e: e.tensor_tensor(out=sm[:, 26:27], in0=PA[:, 0, 128:129], in1=sm[:, 34:35], op=ALU.mult))
            S(lambda e: e.tensor_tensor(out=sm[:, 27:28], in0=PA[:, 1, 128:129], in1=sm[:, 34:35], op=ALU.mult))
            powers(PW, "PW", 26, 27, 7)
            S1 = pg[:, 3:19]; S1s = pg[:, 19:35]; C1r = pg[:, 35:51]; C2r = pg[:, 51:67]
            bk = [("bb", i) for i in range(12)]
            V("vector", lambda e, S1s=S1s: e.tensor_scalar(out=bb[:, 0, :], in0=S1s, scalar1=sgnA[:, 0:1], scalar2=None, op0=ALU.mult),
              ["prm", "sgnA"], [bk[0]])
            V("vector", lambda e, S1=S1: e.tensor_scalar(out=bb[:, 1, :], in0=S1, scalar1=sgnB[:, 0:1], scalar2=None, op0=ALU.mult),
              ["prm", "sgnB"], [bk[1]])
            cre, cim = sm[:, 8:9], sm[:, 9:10]
            V("vector", lambda e, S1=S1: e.tensor_scalar(out=bb[:, 2, :], in0=S1, scalar1=cre, scalar2=None, op0=ALU.mult), ["prm", "sm"], [bk[2]])
            V("vector", lambda e: e.scalar_tensor_tensor(out=bb[:, 2, :], in0=bb[:, 0, :], scalar=cim, in1=bb[:, 2, :], op0=ALU.mult, op1=ALU.add),
              [bk[0], bk[2], "sm"], [bk[2]])
            V("vector", lambda e, S1=S1: e.tensor_scalar(out=bb[:, 3, :], in0=S1, scalar1=cim, scalar2=-1.0, op0=ALU.mult, op1=ALU.mult),
              ["prm", "sm"], [bk[3]])
            V("vector", lambda e: e.scalar_tensor_tensor(out=bb[:, 3, :], in0=bb[:, 0, :], scalar=cre, in1=bb[:, 3, :], op0=ALU.mult, op1=ALU.add),
              [bk[0], bk[3], "sm"], [bk[3]])
            V("vector", lambda e, S1s=S1s: e.tensor_scalar(out=bb[:, 4, :], in0=S1s, scalar1=cre, scalar2=None, op0=ALU.mult), ["prm", "sm"], [bk[4]])
            V("vector", lambda e: e.scalar_tensor_tensor(out=bb[:, 4, :], in0=bb[:, 1, :], scalar=cim, in1=bb[:, 4, :], op0=ALU.mult, op1=ALU.add),
              [bk[1], bk[4], "sm"], [bk[4]])
            V("vector", lambda e, S1s=S1s: e.tensor_scalar(out=bb[:, 5, :], in0=S1s, scalar1=cim, scalar2=-1.0, op0=ALU.mult, op1=ALU.mult),
              ["prm", "sm"], [bk[5]])
            V("vector", lambda e: e.scalar_tensor_tensor(out=bb[:, 5, :], in0=bb[:, 1, :], scalar=cre, in1=bb[:, 5, :], op0=ALU.mult, op1=ALU.add),
              [bk[1], bk[5], "sm"], [bk[5]])
            V("vector", lambda e, C1r=C1r: e.tensor_scalar(out=bb[:, 6, :], in0=C1r, scalar1=sgnB[:, 0:1], scalar2=None, op0=ALU.mult),
              ["prm", "sgnB"], [bk[6]])
            V("vector", lambda e, C2r=C2r: e.tensor_scalar(out=bb[:, 7, :], in0=C2r, scalar1=-1.0, scalar2=None, op0=ALU.mult), ["prm"], [bk[7]])
            cmul_small(8, 9, 2, 3, 22, 23)
            cmul_small(10, 11, 4, 5, 22, 23)
            table(TBr[:, :, :], "TBr", PN, "PN", 128, 8, 9, "vector")
            table(TBrs[:, :, :], "TBrs", PN, "PN", 128, 10, 11, "gpsimd")
            table(TBn[:, :, :], "TBn", PN, "PN", 8, 2, 3, "vector")
            table(TC[:, :, :], "TC", PA, "PA", 129, 6, 7, "gpsimd")

            for src, skey, dst, dkey, pb, eng in ((TBr, "TBr", Wst, "Wst", 0, "scalar"), (TBrs, "TBrs", Wsts, "Wsts", 1, "vector")):
                for half in range(2):
                    for jj in range(8):
                        j = half * 8 + jj
                        P.op("tensor", (lambda e, src=src, j=j, jj=jj, pb=pb: e.transpose(
                            out=bf2[pb][:, jj * 128:(jj + 1) * 128], in_=src[:, :, :].rearrange("p m h -> p (m h)")[:, 128 * j:128 * j + 128], identity=ident[:, :])),
                            reads=[skey, "ident"], writes=[("bf2", pb)])
                    if eng == "scalar":
                        P.op("scalar", lambda e, dst=dst, half=half, pb=pb: e.copy(
                            out=dst[:, 8 * half:8 * half + 8, :], in_=bf2[pb][:, :].rearrange("p (j m) -> p j m", j=8)),
                            reads=[("bf2", pb)], writes=[(dkey, half)])
                    else:
                        P.op("vector", lambda e, dst=dst, half=half, pb=pb: e.tensor_copy(
                            out=dst[:, 8 * half:8 * half + 8, :], in_=bf2[pb][:, :].rearrange("p (j m) -> p j m", j=8)),
                            reads=[("bf2", pb)], writes=[(dkey, half)])
            for rnd in range(2):
                for dd in range(8):
                    d = rnd * 8 + dd
                    pt = big[2 + dd // 4]
                    P.op("tensor", (lambda e, d=d, dd=dd, pt=pt: e.matmul(pt[:, (dd % 4) * 128:(dd % 4 + 1) * 128], lhsT=TBn[:, :, :].rearrange("p s h -> p (s h)"),
                                                                        rhs=TC[:, :, :].rearrange("p k h -> p (k h)")[:, 128 * d:128 * d + 128], start=True, stop=True)),
                         reads=["TBn", "TC"], writes=[("big", 2 + dd // 4)])
                for q in range(2):
                    d0 = rnd * 8 + q * 4
                    if rnd == 0 and q == 0:
                        P.op("vector", lambda e: e.tensor_tensor(out=Kt[:, 0, :], in0=big[2][:, 0:128], in1=cs["mask0"][:, :], op=ALU.mult),
                             reads=[("big", 2), "mask0"], writes=[("Kt", 0)])
                        P.op("vector", lambda e: e.tensor_copy(out=Kt[:, 1:4, :], in_=big[2][:, 128:512].rearrange("p (d m) -> p d m", d=3)),
                             reads=[("big", 2)], writes=[("Kt", 1)])
                    else:
                        P.op("scalar", lambda e, d0=d0, q=q: e.copy(out=Kt[:, d0:d0 + 4, :],
                                                                    in_=big[2 + q][:, :].rearrange("p (d m) -> p d m", d=4)),
                             reads=[("big", 2 + q)], writes=[("Kt", 2 + rnd * 2 + q)])
            ktkeys = [("Kt", i) for i in range(6) if i != 2] + [("Kt", 2)]
            ktkeys = [("Kt", 0), ("Kt", 1), ("Kt", 3), ("Kt", 4), ("Kt", 5)]
            V("vector", lambda e: e.tensor_copy(out=r128t[:, :], in_=sm[:, 25:26].to_broadcast([128, 128])), ["sm"], ["r128t"])
            V("vector", lambda e: e.tensor_scalar(out=sgs[:, 0, :], in0=PW[:, 1, 0:128], scalar1=sgnB[:, 0:1], scalar2=None, op0=ALU.mult),
              ["PW", "sgnB"], ["sgs0"])
            V("vector", lambda e: e.tensor_scalar(out=sgs[:, 1, :], in0=PW[:, 1, 0:128], scalar1=sgnA[:, 0:1], scalar2=None, op0=ALU.mult),
              ["PW", "sgnA"], ["sgs1"])

            ukeys = [("U", g, nt) for nt in range(NNT)]
            Ug = U[:, g, :]
            for w_, wkey, col in ((Wst, "Wst", 0), (Wsts, "Wsts", 1)):
                for j in range(16):
                    P.op("tensor", (lambda e, w_=w_, j=j, col=col, Ug=Ug: e.matmul(lcb[:, col * NCH:(col + 1) * NCH], lhsT=w_[:, j, :],
                                                                                 rhs=Ug[:, j::16], start=(j == 0), stop=(j == 15))),
                         reads=ukeys + [(wkey, 0), (wkey, 1)], writes=[("lcb", col)])
            V("scalar", lambda e: e.copy(out=lcs[:, :, :], in_=lcb[:, 0:2 * NCH].rearrange("p (a c) -> p a c", a=2)),
              [("lcb", 0), ("lcb", 1)], ["lcs"])
            cosN = PW[:, 0, 0:NCH]
            V("vector", lambda e, cosN=cosN: e.tensor_tensor(out=rot[:, 4, :], in0=lcs[:, 0, :], in1=cosN, op=ALU.mult), ["lcs", "PW"], ["rot4"])
            V("vector", lambda e: e.tensor_tensor(out=rot[:, 0, :], in0=lcs[:, 1, :], in1=sgs[:, 0, 0:NCH], op=ALU.mult), ["lcs", "sgs0"], ["rot0"])
            V("vector", lambda e: e.tensor_tensor(out=rot[:, 0, :], in0=rot[:, 0, :], in1=rot[:, 4, :], op=ALU.add), ["rot0", "rot4"], ["rot0"])
            V("vector", lambda e, cosN=cosN: e.tensor_tensor(out=rot[:, 4, :], in0=lcs[:, 1, :], in1=cosN, op=ALU.mult), ["lcs", "PW", "rot4"], ["rot4"])
            V("vector", lambda e: e.tensor_tensor(out=rot[:, 1, :], in0=lcs[:, 0, :], in1=sgs[:, 1, 0:NCH], op=ALU.mult), ["lcs", "sgs1"], ["rot1"])
            V("vector", lambda e: e.tensor_tensor(out=rot[:, 1, :], in0=rot[:, 1, :], in1=rot[:, 4, :], op=ALU.add), ["rot1", "rot4"], ["rot1"])
            V("vector", lambda e: e.tensor_tensor_scan(out=rot[:, 2, :], data0=r128t[:, 0:NCH], data1=rot[:, 0, :], initial=0.0,
                                                       op0=ALU.mult, op1=ALU.add), ["r128t", "rot0"], ["rot2"])
            V("vector", lambda e: e.tensor_tensor_scan(out=rot[:, 3, :], data0=r128t[:, 0:NCH], data1=rot[:, 1, :], initial=0.0,
                                                       op0=ALU.mult, op1=ALU.add), ["r128t", "rot1"], ["rot3"])
            V("vector", lambda e, cosN=cosN: e.tensor_tensor(out=rot[:, 4, :], in0=rot[:, 2, :], in1=cosN, op=ALU.mult), ["rot2", "PW", "rot4"], ["rot4"])
            V("vector", lambda e: e.tensor_tensor(out=rot[:, 5, :], in0=rot[:, 3, :], in1=sgs[:, 1, 0:NCH], op=ALU.mult), ["rot3", "sgs1"], ["rot5"])
            V("vector", lambda e: e.tensor_tensor(out=rot[:, 5, :], in0=rot[:, 5, :], in1=rot[:, 4, :], op=ALU.add), ["rot5", "rot4"], ["rot5"])
            V("vector", lambda e: e.memset(xcin[:, 0:1], 0.0), [], ["xcin"])
            if NCH > 1:
                V("vector", lambda e: e.tensor_copy(out=xcin[:, 1:NCH], in_=rot[:, 5, 0:NCH - 1]), ["rot5", "xcin"], ["xcin"])
            dvec = pg[:, 67:68]
            for nt in range(NNT):
                yb_ = ny % 2
                ny += 1
                yps = big[yb_]
                y3 = yps[:, 0:CT * 16].rearrange("p (c j) -> p c j", j=16)
                u3 = Ug[:, nt * CT * 16:(nt + 1) * CT * 16].rearrange("p (c j) -> p c j", j=16)
                for d in range(16):
                    P.op("tensor", (lambda e, d=d, y3=y3, u3=u3: e.matmul(y3[:, :, d:16], lhsT=Kt[:, d, :], rhs=u3[:, :, 0:16 - d],
                                                                        start=(d == 0), stop=False)),
                         reads=[("U", g, nt)] + ktkeys, writes=[("big", yb_)])
                for j in range(16):
                    P.op("tensor", (lambda e, j=j, y3=y3, nt=nt: e.matmul(y3[:, :, j], lhsT=TC[:, :, :].rearrange("p k h -> p (k h)")[:, 128 * j + 16:128 * j + 144],
                                                                        rhs=xcin[:, nt * CT:(nt + 1) * CT], start=False, stop=(j == 15))),
                         reads=["TC", "xcin"], writes=[("big", yb_)])
                yt = ytmp[yb_]
                V("vector", lambda e, yt=yt, yps=yps, nt=nt, Ug=Ug, dvec=dvec: e.scalar_tensor_tensor(
                    out=yt[:, :], in0=Ug[:, nt * CT * 16:(nt + 1) * CT * 16], scalar=dvec, in1=yps[:, 0:CT * 16], op0=ALU.mult, op1=ALU.add),
                  [("big", yb_), ("U", g, nt), "prm"], [("ytmp", yb_)])
                V("scalar", lambda e, yt=yt, nt=nt, Ug=Ug: e.activation(out=Ug[:, nt * CT * 16:(nt + 1) * CT * 16], in_=yt[:, :], func=GELU_FUNC),
                  [("ytmp", yb_)], [("U", g, nt)])
        yS = [P.sb([128, 8, 256], BF16, name=f"yS{i}") for i in range(2)]
        for sc in range(NSC):
            sb_ = sc % 2
            nt = (sc * 128) // (CT * 16)
            for g in range(16):
                P.op("tensor", (lambda e, g=g, sc=sc: e.transpose(out=bf2[g // 8][:, (g % 8) * 128:(g % 8 + 1) * 128],
                                                                 in_=U[:, g, sc * 128:(sc + 1) * 128], identity=ident[:, :])),
                     reads=[("U", g, nt), "ident"], writes=[("bf2", g // 8)])
            for hf in range(2):
                src = bf2[hf][:, :].rearrange("p (g t h) -> p t g h", g=8, t=8, h=16)
                dst = yS[sb_][:, :, 128 * hf:128 * hf + 128].rearrange("p t (g h) -> p t g h", g=8)
                if hf == 0:
                    P.op("scalar", lambda e, src=src, dst=dst: e.copy(out=dst, in_=src), reads=[("bf2", hf)], writes=[("yS", sb_, hf)])
                else:
                    P.op("vector", lambda e, src=src, dst=dst: e.tensor_copy(out=dst, in_=src), reads=[("bf2", hf)], writes=[("yS", sb_, hf)])
            P.dma("gpsimd", y_d[sc * 1024:(sc + 1) * 1024, :].rearrange("(j t) c -> j t c", t=8), yS[sb_][:, :, :],
                  reads=[("yS", sb_, 0), ("yS", sb_, 1)])
        P.emit()
    return nc


from concourse.bass_utils import run_bass_kernel_spmd

D_MODEL = 1024
SEQ = 16384
BATCH = 2
O1 = 2048
O2 = O1 + 1024
O3 = O2 + 1536
_PROG_CACHE = {}


def _prog(name, fn, *a):
    key = (name,) + a
    if key not in _PROG_CACHE:
        _PROG_CACHE[key] = fn(*a)
    return _PROG_CACHE[key]


def _run(nc, in_maps):
    res = run_bass_kernel_spmd(nc, in_maps, core_ids=list(range(8)))
    return res.results


def _c(a):
    return np.ascontiguousarray(a)


def kernel(x, layer_norm_w, ab_w_in, s5_lam_re, s5_lam_im, s5_log_dt, s5_b_re, s5_b_im, s5_c_re, s5_c_im, s5_d,
           s5_glu_w, s5_glu_b, ssd_conv_w, ssd_conv_b, ssd_dt_bias, ssd_a_log, ssd_d, ssd_norm_w, ab_w_out,
           ret_w_in, ret_gn_w, ret_gn_b, ret_w_out, final_norm_w):
    f32 = np.float32
    x = np.asarray(x, f32)
    L = x.shape[1]
    w_in = np.asarray(ab_w_in, f32)[0]
    ident = np.eye(128, dtype=f32)
    lnw0 = _c(np.asarray(layer_norm_w, f32)[0])
    lnw1 = _c(np.asarray(layer_norm_w, f32)[1])
    cores = [(c // 4, c % 4) for c in range(8)]

    sc = s5_consts()
    maps = []
    for b, j in cores:
        prm = s5_pack_params(np.asarray(s5_lam_re, f32)[0], np.asarray(s5_lam_im, f32)[0], np.asarray(s5_log_dt, f32)[0],
                             np.asarray(s5_b_re, f32)[0], np.asarray(s5_b_im, f32)[0], np.asarray(s5_c_re, f32)[0],
                             np.asarray(s5_c_im, f32)[0], np.asarray(s5_d, f32)[0], list(range(16 * j, 16 * j + 16)))
        m = dict(x=_c(x[b]), lnw=lnw0, w=_c(w_in[:, O1 + 256 * j:O1 + 256 * (j + 1)]), prm=prm, ident_in=ident)
        m.update(sc)
        maps.append(m)
    r = _run(_prog("s5", build_s5, L), maps)
    ya = np.zeros((BATCH, L, 1024), ml_dtypes.bfloat16)
    for c, (b, j) in enumerate(cores):
        ya[b, :, 256 * j:256 * (j + 1)] = r[c]["y"]

    ssc = ssd_consts()
    cw_full = np.asarray(ssd_conv_w, f32)[0]
    cb_full = np.asarray(ssd_conv_b, f32)[0]
    maps = []
    for b, j in cores:
        g = j // 2
        cols = np.concatenate([np.arange(256 * j, 256 * (j + 1)), 1024 + np.arange(128 * g, 128 * (g + 1)),
                               1280 + np.arange(128 * g, 128 * (g + 1))])
        W = np.concatenate([w_in[:, O2 + cols], w_in[:, O3 + 4 * j:O3 + 4 * (j + 1)]], 1)
        cw = np.zeros((128, 16), f32)
        cbv = np.zeros((128, 4), f32)
        for ct in range(4):
            cc = cols[ct * 128:(ct + 1) * 128]
            for t in range(4):
                cw[:, ct * 4 + t] = cw_full[t, cc]
            cbv[:, ct] = cb_full[cc]
        m = dict(x=_c(x[b]), lnw=lnw0, w=_c(W), cw=cw, cb=cbv,
                 dtb=_c(np.asarray(ssd_dt_bias, f32)[0, 4 * j:4 * (j + 1)]), alog=_c(np.asarray(ssd_a_log, f32)[0, 4 * j:4 * (j + 1)]),
                 dsk=_c(np.asarray(ssd_d, f32)[0, 4 * j:4 * (j + 1)]), tri=ssc["tri"], ones=ssc["ones"], ident_in=ident)
        maps.append(m)
    r = _run(_prog("ssd", build_ssd, L), maps)
    yb = np.zeros((BATCH, L, 1024), ml_dtypes.bfloat16)
    for c, (b, j) in enumerate(cores):
        yb[b, :, 256 * j:256 * (j + 1)] = r[c]["y"]

    T = L // 4
    maps = []
    for b, q in cores:
        ts = slice(q * T, (q + 1) * T)
        maps.append(dict(x=_c(x[b, ts]), ya=_c(ya[b, ts]), yb=_c(yb[b, ts]), lnw0=lnw0, lnw1=lnw1, wz=_c(w_in[:, 0:2048]),
                         gluw=_c(np.asarray(s5_glu_w, f32)[0]), glub=_c(np.asarray(s5_glu_b, f32)[0]),
                         snw=_c(np.asarray(ssd_norm_w, f32)[0]), wout=_c(np.asarray(ab_w_out, f32)[0]), ident_in=ident))
    r = _run(_prog("outB", build_outB, T), maps)
    h1 = np.zeros((BATCH, L, 1024), f32)
    hn1 = np.zeros((BATCH, L, 1024), ml_dtypes.bfloat16)
    for c, (b, q) in enumerate(cores):
        h1[b, q * T:(q + 1) * T] = r[c]["h1"]
        hn1[b, q * T:(q + 1) * T] = r[c]["hn1"]

    rw = np.asarray(ret_w_in, f32)[0]
    cosT, sinT = rope_tables(L)
    perm = np.concatenate([np.arange(0, 256, 2), np.arange(1, 256, 2)])
    maps = []
    for b, h in cores:
        Wc = np.concatenate([rw[:, 256 * h:256 * (h + 1)][:, perm], rw[:, 1024 + 256 * h:1024 + 256 * (h + 1)][:, perm],
                             rw[:, 2048 + 512 * h:2048 + 512 * (h + 1)], rw[:, 4096 + 512 * h:4096 + 512 * (h + 1)]], 1)
        rc = ret_consts(h)
        maps.append(dict(hn=_c(hn1[b]), w=_c(Wc), cosT=cosT, sinT=sinT, intraT=rc["intraT"], gq=rc["gq"], kdec=rc["kdec"], cdv=rc["cdv"],
                         gnw=_c(np.asarray(ret_gn_w, f32)[0, 512 * h:512 * (h + 1)]), gnb=_c(np.asarray(ret_gn_b, f32)[0, 512 * h:512 * (h + 1)]),
                         ident_in=ident))
    r = _run(_prog("ret", build_ret, L), maps)
    y1 = np.zeros((BATCH, L, 2048), ml_dtypes.bfloat16)
    for c, (b, h) in enumerate(cores):
        y1[b, :, 512 * h:512 * (h + 1)] = r[c]["y"]

    maps = []
    for b, q in cores:
        ts = slice(q * T, (q + 1) * T)
        maps.append(dict(y=_c(y1[b, ts]), h=_c(h1[b, ts]), wout=_c(np.asarray(ret_w_out, f32)[0]), fnw=_c(np.asarray(final_norm_w, f32)),
                         ident_in=ident))
    r = _run(_prog("outD", build_outD, T), maps)
    out = np.zeros((BATCH, L, 1024), f32)
    for c, (b, q) in enumerate(cores):
        out[b, q * T:(q + 1) * T] = r[c]["out"]
    return out
```
